# Optimizing a Trainium2 kernel written in Bass

```python
import math
import jax, jax.numpy as jnp
from jax import lax
import numpy as np

D_MODEL = 1024
BATCH = 8
SEQ = 2048
DEPTH = 1
DEC_BATCH = 128
DEC_SEQ = 1
PAST_LEN = 16384
PAGE_SIZE = 128

MIX_W = D_MODEL
GROUP_W = MIX_W // 2
RET_HEADS = 4
RET_DK = GROUP_W // RET_HEADS
RET_DV = GROUP_W // RET_HEADS
HG_HEADS = 4
HG_DK = GROUP_W // HG_HEADS
HG_DV = GROUP_W // HG_HEADS
N_IN_COLS = 8 * GROUP_W
CHUNK = 64
ROPE_BASE = 10000.0
NORM_EPS = 1e-6

kernel_name = "retention_hgrn2_parallel_heads_step"


def rmsnorm(x, g):
    xf = x.astype(jnp.float32)
    y = xf * lax.rsqrt(jnp.mean(xf * xf, axis=-1, keepdims=True) + NORM_EPS)
    return (y * g.astype(jnp.float32)).astype(x.dtype)


def head_rmsnorm(o, g):
    B, L, H, dv = o.shape
    y = o * lax.rsqrt(jnp.mean(o * o, axis=-1, keepdims=True) + NORM_EPS)
    return y.reshape(B, L, H * dv) * g.astype(jnp.float32)


def rotary(x, pos):
    half = x.shape[-1] // 2
    freqs = 1.0 / (ROPE_BASE ** (jnp.arange(half, dtype=jnp.float32) / half))
    ang = pos[:, None] * freqs[None, :]
    c = jnp.cos(ang)[None, :, None, :]
    s = jnp.sin(ang)[None, :, None, :]
    x1, x2 = x[..., :half], x[..., half:]
    return jnp.concatenate([x1 * c - x2 * s, x1 * s + x2 * c], axis=-1)


def chunked_decay_recurrence(q, k, v, logf, s0):
    B, L, H, DK = q.shape
    DV = v.shape[-1]
    C = math.gcd(L, CHUNK)
    n = L // C

    def to_chunks(a):
        return a.astype(jnp.float32).reshape(B, n, C, H, a.shape[-1]).transpose(1, 0, 3, 2, 4)

    qs, ks, vs, gs = to_chunks(q), to_chunks(k), to_chunks(v), to_chunks(logf)
    mask = jnp.tril(jnp.ones((C, C), dtype=bool))[:, :, None]
    scalar_decay = logf.shape[-1] == 1

    def step(S, inp):
        qc, kc, vc, gc = inp
        b = jnp.cumsum(gc, axis=2)
        o_inter = jnp.einsum('bhtd,bhdv->bhtv', qc * jnp.exp(b), S)
        diff = b[:, :, :, None, :] - b[:, :, None, :, :]
        decay = jnp.exp(jnp.where(mask, diff, -jnp.inf))
        if scalar_decay:
            A = jnp.einsum('bhtd,bhsd->bhts', qc, kc) * decay[..., 0]
        else:
            A = jnp.einsum('bhtd,bhsd,bhtsd->bhts', qc, kc, decay)
        o = o_inter + jnp.einsum('bhts,bhsv->bhtv', A, vc)
        bC = b[:, :, -1:, :]
        S_new = jnp.exp(bC[:, :, 0, :])[..., None] * S + jnp.einsum(
            'bhsd,bhsv->bhdv', kc * jnp.exp(bC - b), vc)
        return S_new, o

    S, o = lax.scan(step, s0.astype(jnp.float32), (qs, ks, vs, gs))
    o = o.transpose(1, 0, 3, 2, 4).reshape(B, L, H, DV)
    return o, S


def mixer_layer(x, pos, s_ret, s_hg, norm_g, w_in, ret_g, hg_g, lb, w_out):
    B, L, _ = x.shape
    h = rmsnorm(x, norm_g)
    p = (h @ w_in).astype(jnp.float32)
    rq, rk, rv, rg, hq, hf, hi, hgate = jnp.split(p, 8, axis=-1)

    rq = rotary(rq.reshape(B, L, RET_HEADS, RET_DK), pos)
    rk = rotary(rk.reshape(B, L, RET_HEADS, RET_DK), pos) * (RET_DK ** -0.5)
    rv = rv.reshape(B, L, RET_HEADS, RET_DV)
    log_gamma = jnp.log1p(-jnp.exp2(-5.0 - jnp.arange(RET_HEADS, dtype=jnp.float32)))
    log_gamma = jnp.broadcast_to(log_gamma[None, None, :, None], (B, L, RET_HEADS, 1))
    o_ret, s_ret_new = chunked_decay_recurrence(rq, rk, rv, log_gamma, s_ret)
    o_ret = head_rmsnorm(o_ret, ret_g) * jax.nn.silu(rg)

    lbf = lb.astype(jnp.float32)
    logf = jnp.logaddexp(jnp.log(lbf), jnp.log1p(-lbf) + jax.nn.log_sigmoid(hf))
    kin = (1.0 - lbf) * jax.nn.sigmoid(-hf)
    hq_act = jax.nn.silu(hq)
    o_hg, s_hg_new = chunked_decay_recurrence(
        hq_act.reshape(B, L, HG_HEADS, HG_DK), kin.reshape(B, L, HG_HEADS, HG_DK),
        hi.reshape(B, L, HG_HEADS, HG_DV), logf.reshape(B, L, HG_HEADS, HG_DK), s_hg)
    o_hg = head_rmsnorm(o_hg, hg_g) * jax.nn.silu(hgate)

    o_cat = jnp.concatenate([o_ret, o_hg], axis=-1).astype(x.dtype)
    y = x + o_cat @ w_out
    return y, s_ret_new.astype(s_ret.dtype), s_hg_new.astype(s_hg.dtype)


def setup_inputs(seed: int = 0) -> dict:
    key = jax.random.key(seed)
    ks = jax.random.split(key, 12)
    f32 = jnp.float32
    return {
        "x_prompt": jax.random.normal(ks[0], (BATCH, SEQ, D_MODEL), f32),
        "x_sample": jax.random.normal(ks[1], (DEC_BATCH, DEC_SEQ, D_MODEL), f32),
        "state_ret": 0.3 * jax.random.normal(ks[2], (DEPTH, DEC_BATCH, RET_HEADS, RET_DK, RET_DV), f32),
        "state_hgrn": 0.3 * jax.random.normal(ks[3], (DEPTH, DEC_BATCH, HG_HEADS, HG_DK, HG_DV), f32),
        "norm_g": 1.0 + 0.01 * jax.random.normal(ks[4], (DEPTH, D_MODEL), f32),
        "w_in": jax.random.normal(ks[5], (DEPTH, D_MODEL, N_IN_COLS), f32) * D_MODEL ** -0.5,
        "ret_norm_g": 1.0 + 0.01 * jax.random.normal(ks[6], (DEPTH, GROUP_W), f32),
        "hg_norm_g": 1.0 + 0.01 * jax.random.normal(ks[7], (DEPTH, GROUP_W), f32),
        "hg_lb": 0.5 * jax.random.normal(ks[8], (DEPTH + 1, GROUP_W), f32),
        "w_out": jax.random.normal(ks[9], (DEPTH, 2 * GROUP_W, D_MODEL), f32) * (2 * GROUP_W) ** -0.5 * 0.5,
        "final_norm_g": 1.0 + 0.01 * jax.random.normal(ks[10], (D_MODEL,), f32),
    }


def reference(x_prompt, x_sample, state_ret, state_hgrn, norm_g, w_in, ret_norm_g, hg_norm_g,
              hg_lb, w_out, final_norm_g):
    Bp, Lp, _ = x_prompt.shape
    Bs, Ls, _ = x_sample.shape
    pos_prompt = jnp.arange(Lp, dtype=jnp.float32)
    pos_sample = PAST_LEN + jnp.arange(Ls, dtype=jnp.float32)
    lb_all = jnp.cumsum(jax.nn.softmax(hg_lb.astype(jnp.float32), axis=0), axis=0)

    hp, hs = x_prompt, x_sample
    rp_list, hp_list, rs_list, hs_list = [], [], [], []
    for l in range(DEPTH):
        zero_ret = jnp.zeros((Bp, RET_HEADS, RET_DK, RET_DV), state_ret.dtype)
        zero_hg = jnp.zeros((Bp, HG_HEADS, HG_DK, HG_DV), state_hgrn.dtype)
        hp, r_p, g_p = mixer_layer(hp, pos_prompt, zero_ret, zero_hg, norm_g[l], w_in[l],
                                   ret_norm_g[l], hg_norm_g[l], lb_all[l], w_out[l])
        hs, r_s, g_s = mixer_layer(hs, pos_sample, state_ret[l], state_hgrn[l], norm_g[l], w_in[l],
                                   ret_norm_g[l], hg_norm_g[l], lb_all[l], w_out[l])
        rp_list.append(r_p); hp_list.append(g_p); rs_list.append(r_s); hs_list.append(g_s)

    y_prompt = rmsnorm(hp, final_norm_g)
    y_sample = rmsnorm(hs, final_norm_g)
    new_ret_prompt = jnp.stack(rp_list, axis=0)
    new_hgrn_prompt = jnp.stack(hp_list, axis=0)
    new_ret_sample = jnp.stack(rs_list, axis=0)
    new_hgrn_sample = jnp.stack(hs_list, axis=0)
    return (y_prompt, y_sample, new_ret_prompt, new_hgrn_prompt, new_ret_sample, new_hgrn_sample)
```

```python
import numpy as np
import ml_dtypes
from contextlib import ExitStack

import concourse.bass as bass
import concourse.mybir as mybir
from concourse.bass_utils import run_bass_kernel_spmd

F32 = mybir.dt.float32
BF16 = mybir.dt.bfloat16
AF = mybir.ActivationFunctionType
ALU = mybir.AluOpType
AX = mybir.AxisListType

D = 1024
L = 2048
NTILE = 16
NS = 16
EPS = 1e-6
NCORES = 8
USE_LIST_SCHED = True

FAC_OFF = 0
FAC_N = 16
ERET_OFF = FAC_OFF + FAC_N
ERET_N = 8
GAM_OFF = ERET_OFF + ERET_N
GAM_N = 4
MASK_OFF = GAM_OFF + GAM_N
EYE_OFF = MASK_OFF + 128
IDF_OFF = EYE_OFF + 256
NCF = IDF_OFF + 128


def _make_consts():
    cf = np.zeros((128, NCF), np.float32)
    half = 64
    freqs = 1.0 / (10000.0 ** (np.arange(half, dtype=np.float64) / half))
    p = np.arange(128)
    cs = np.zeros((128, 17, 2, 64), np.float32)
    for i in range(17):
        pos = (i * 128 + p).astype(np.float64) if i < 16 else np.full(128, 16384.0, np.float64)
        ang = pos[:, None] * freqs[None, :]
        cs[:, i, 0, :] = np.cos(ang)
        cs[:, i, 1, :] = np.sin(ang)
    gam = 1.0 - np.exp2(-5.0 - np.arange(4, dtype=np.float64))
    t = p.astype(np.float64)
    fac = np.zeros((128, 2, 8), np.float64)
    for h in range(4):
        fac[:, 0, h] = gam[h] ** (t - 127.0)
        fac[:, 0, 4 + h] = (128.0 ** -0.5) * gam[h] ** (127.0 - t)
        fac[:, 1, h] = 1.0
        fac[:, 1, 4 + h] = 128.0 ** -0.5
    cf[:, FAC_OFF:FAC_OFF + FAC_N] = fac.reshape(128, -1)
    eret = np.zeros((128, 2, 4), np.float64)
    eret[:, 0, :] = gam[None, :] ** 128.0
    eret[:, 1, :] = gam[None, :] ** 128.0
    cf[:, ERET_OFF:ERET_OFF + ERET_N] = eret.reshape(128, -1)
    cf[:, GAM_OFF:GAM_OFF + 4] = gam[None, :]
    s = p[:, None]
    tt = p[None, :]
    cf[:, MASK_OFF:MASK_OFF + 128] = (s <= tt)
    cb = np.zeros((128, 128 + 260), np.float32)
    cb[:, 0:128] = np.eye(128, dtype=np.float32)
    m1 = np.zeros((128, 128), np.float32)
    m1[(s > 63) & (s <= tt)] = 1.0
    m1[(s > tt) & (s <= 63)] = -1.0
    cb[:, 128:256] = m1
    cb[:, 256:384] = (s > tt)
    cb[:, 384] = (p <= 63)
    cb[:, 385] = 1.0
    cf[:, EYE_OFF:EYE_OFF + 256] = np.eye(16, dtype=np.float32).reshape(1, 256)
    cf[:, IDF_OFF:IDF_OFF + 128] = np.eye(128, dtype=np.float32)
    cs_t = np.ascontiguousarray(cs.reshape(128, 17, 128).transpose(1, 0, 2))
    return cf, cs_t, cb.astype(ml_dtypes.bfloat16)


class Buf:
    __slots__ = ("name", "last_w", "readers")

    def __init__(self, name):
        self.name = name
        self.last_w = None
        self.readers = []


class Op:
    __slots__ = ("eng", "fn", "deps", "sem", "val", "need_inc", "is_dma", "ninc", "name", "alldeps", "cost", "done_lat", "tab", "idx")


DEFAULT_COST = {"pe": 0.24, "act": 0.72, "dve": 0.66, "pool": 1.3, "sp": 0.6}


class Sched:
    ENGS = ("pe", "act", "dve", "pool", "sp")
    ENGATTR = {"pe": "tensor", "act": "scalar", "dve": "vector", "pool": "gpsimd", "sp": "sync"}

    def __init__(self, nc, stack, same_engine_sync=True):
        self.nc = nc
        self.stack = stack
        self.ops = {e: [] for e in self.ENGS}
        self.eng_sem = {e: stack.enter_context(nc.semaphore("sem_" + e)) for e in self.ENGS}
        self.dma_sems = {}
        self.dma_cnt = {}
        self.same_engine_sync = same_engine_sync
        self.nops = 0

    def op(self, eng, fn, r=(), w=(), name="", c=None, tab=None):
        o = Op()
        o.eng = eng; o.fn = fn; o.is_dma = False; o.need_inc = False; o.name = name
        o.sem = None; o.val = None; o.ninc = 1
        o.cost = DEFAULT_COST[eng] if c is None else c
        o.done_lat = 0.0; o.tab = tab; o.idx = self.nops; self.nops += 1
        self._deps(o, r, w)
        self.ops[eng].append(o)
        return o

    def dma(self, eng, fn, key, r=(), w=(), n=1, name="", us=1.5):
        o = Op()
        o.eng = eng; o.fn = fn; o.is_dma = True; o.need_inc = True; o.name = name
        o.cost = 0.45 * n; o.done_lat = us; o.tab = None; o.idx = self.nops; self.nops += 1
        if key not in self.dma_sems:
            self.dma_sems[key] = self.stack.enter_context(self.nc.semaphore("dsem_%d" % len(self.dma_sems)))
            self.dma_cnt[key] = 0
        self.dma_cnt[key] += 16 * n
        o.sem = self.dma_sems[key]; o.val = self.dma_cnt[key]; o.ninc = n
        self._deps(o, r, w)
        self.ops[eng].append(o)
        return o

    def _deps(self, o, r, w):
        deps = []
        for b in r:
            if b.last_w is not None:
                deps.append(b.last_w)
        for b in w:
            if b.last_w is not None:
                deps.append(b.last_w)
            deps.extend(b.readers)
        seen = set(); out = []
        o.alldeps = []
        for d in deps:
            if id(d) in seen or d is o:
                continue
            seen.add(id(d))
            o.alldeps.append(d)
            if (not d.is_dma) and (not o.is_dma) and d.eng == o.eng and (o.eng == "pe" or not self.same_engine_sync):
                continue
            if not d.is_dma:
                d.need_inc = True
            out.append(d)
        o.deps = out
        for b in r:
            b.readers.append(o)
        for b in w:
            b.last_w = o
            b.readers = []

    def schedule(self, sem_lat=0.15, table_cost=1.3):
        import os as _os
        sem_lat = float(_os.environ.get("SCHED_SEMLAT", "1.0"))
        SAME_LAT = float(_os.environ.get("SCHED_SAMELAT", "0.02"))
        allops = []
        for e in self.ENGS:
            allops.extend(self.ops[e])
        allops.sort(key=lambda o: o.idx)
        ndeps = {}
        users = {}
        for o in allops:
            ndeps[id(o)] = len(o.alldeps)
            for d in o.alldeps:
                users.setdefault(id(d), []).append(o)
        fin = {}
        self.sim_start = {}
        self.sim_fin = fin
        blev = {}
        for o in reversed(allops):
            m = 0.0
            for u in users.get(id(o), ()):
                m = max(m, blev[id(u)] + (sem_lat if u.eng != o.eng else 0.0))
            blev[id(o)] = m + o.cost + (o.done_lat + 2.0 if o.is_dma else 0.0)
        EPS_T = float(_os.environ.get("SCHED_EPS", "0.0"))
        ready = {e: [] for e in self.ENGS}
        for o in allops:
            if ndeps[id(o)] == 0:
                ready[o.eng].append((0.0, o))
        free = {e: 0.0 for e in self.ENGS}
        cur_tab = [None]
        dma_free = [0.0]
        order = {e: [] for e in self.ENGS}
        remaining = len(allops)
        WINDOW = int(_os.environ.get("SCHED_WINDOW", "150"))
        sched = set()
        oldest = 0
        while remaining:
            while oldest < len(allops) and id(allops[oldest]) in sched:
                oldest += 1
            best = None
            for e in self.ENGS:
                for (rt, o) in ready[e]:
                    if o.idx > allops[oldest].idx + WINDOW:
                        continue
                    st = max(free[e], rt)
                    if e == "act" and o.tab is not None and o.tab != cur_tab[0]:
                        st += table_cost
                    key = ((round(st / EPS_T) if EPS_T > 0 else st), -blev[id(o)] if EPS_T > 0 else 0.0, o.idx)
                    if best is None or key < best[0]:
                        best = (key, e, rt, o)
            assert best is not None
            e, rt, o = best[1], best[2], best[3]
            st = max(free[e], rt)
            if e == "act" and o.tab is not None and o.tab != cur_tab[0]:
                st += table_cost
            ready[e].remove((rt, o))
            if e == "act" and o.tab is not None:
                cur_tab[0] = o.tab
            self.sim_start[id(o)] = st
            free[e] = st + o.cost
            if o.is_dma:
                tx0 = max(st + o.cost, dma_free[0])
                dma_free[0] = tx0 + o.done_lat
                fin[id(o)] = dma_free[0] + 2.0
            else:
                fin[id(o)] = st + o.cost
            order[e].append(o)
            sched.add(id(o))
            remaining -= 1
            for u in users.get(id(o), ()):
                ndeps[id(u)] -= 1
                if ndeps[id(u)] == 0:
                    rt_u = max(fin[id(d)] + (sem_lat if d.eng != u.eng else SAME_LAT) for d in u.alldeps)
                    ready[u.eng].append((rt_u, u))
        self.ops = order
        return max(free.values())

    def emit(self, block, final_waits=()):
        for e in self.ENGS:
            c = 0
            for o in self.ops[e]:
                if o.is_dma:
                    continue
                if o.need_inc:
                    c += 1
                    o.sem = self.eng_sem[e]; o.val = c
        finals = [(o.sem, o.val) for o in final_waits]
        for e in self.ENGS:
            ops = self.ops[e]

            def body(eng, ops=ops, e=e):
                known = {}
                for o in ops:
                    need = {}
                    for d in o.deps:
                        k = id(d.sem)
                        if known.get(k, 0) >= d.val:
                            continue
                        if k not in need or need[k][1] < d.val:
                            need[k] = (d.sem, d.val)
                    for k, (s, v) in need.items():
                        eng.wait_ge(s, v)
                        known[k] = v
                    res = o.fn(eng)
                    if o.is_dma:
                        assert len(res) == o.ninc, (o.name, len(res), o.ninc)
                        for ins in res:
                            ins.then_inc(o.sem, 16)
                    elif o.need_inc:
                        res.then_inc(o.sem, 1)
                if e == "sp":
                    best = {}
                    for (s, v) in finals:
                        if id(s) not in best or best[id(s)][1] < v:
                            best[id(s)] = (s, v)
                    for (s, v) in best.values():
                        eng.wait_ge(s, v)

            getattr(block, self.ENGATTR[e])(body)


def build_program(n_tiles=NTILE, do_sample=True):
    nc = bass.Bass("TRN2", target_bir_lowering=False)

    def din(name, shape, dt=F32):
        return nc.dram_tensor(name, shape, dt, kind="ExternalInput").ap()

    def dout(name, shape, dt=F32):
        return nc.dram_tensor(name, shape, dt, kind="ExternalOutput").ap()

    xp = din("xp", [L, D])
    xs = din("xs", [NS, D])
    sret = din("sret", [NS, 4, 128, 128])
    shg = din("shg", [NS, 4, 128, 128])
    w_in = din("w_in", [D, 4096])
    w_out = din("w_out", [D, D])
    ng = din("ng", [128, 8])
    gn = din("gn", [128, 8])
    hglb = din("hglb", [2, 512])
    fng = din("fng", [1, D])
    cfd = din("cf", [128, NCF])
    csd = din("cs", [17, 128, 128])
    cbd = din("cb", [128, 388], BF16)

    yp = dout("yp", [L, D])
    ys = dout("ys", [NS, D])
    nrp = dout("nrp", [4, 128, 128])
    nhp = dout("nhp", [4, 128, 128])
    nrs = dout("nrs", [NS, 4, 128, 128])
    nhs = dout("nhs", [NS, 4, 128, 128])

    with ExitStack() as st:
        S = Sched(nc, st)
        bufs = {}

        def sb(name, shape, dt=F32):
            t = st.enter_context(nc.sbuf_tensor(name, shape, dt))
            bufs[name] = Buf(name)
            return t

        def ps(name, shape, dt=F32):
            t = st.enter_context(nc.psum_tensor(name, shape, dt))
            bufs[name] = Buf(name)
            return t

        def B(name):
            return bufs[name]

        w_in_bf = sb("w_in_bf", [128, 8, 4096], BF16)
        for g in range(8):
            bufs["win%d" % g] = Buf("win%d" % g)
        w_out_bf = sb("w_out_bf", [128, 8, 1024], BF16)
        for g in range(2):
            bufs["wout%d" % g] = Buf("wout%d" % g)
        wst = [sb("wst%d" % i, [128, 8, 512], F32) for i in range(2)]
        cf = sb("cf_sb", [128, NCF], F32)
        bufs["cf"] = bufs["cf_sb"]
        cs_sb = [sb("cs_sb%d" % i, [128, 2, 64], F32) for i in range(2)]
        cb_sb = sb("cb_sb", [128, 388], BF16)
        bufs["ident_bf"] = bufs["cb_sb"]; bufs["M12_bf"] = bufs["cb_sb"]
        ident_bf = cb_sb[:, 0:128]
        M12_bf = cb_sb[:, 128:388]
        mhalf = sb("mhalf", [128, 8], F32)
        ng_sb = sb("ng_sb", [128, 8], F32)
        gn_sb = sb("gn_sb", [128, 8], F32)
        negc1 = sb("negc1", [128, 512], F32)
        fng_bc = sb("fng_bc", [128, D], F32)
        NX = 4
        x_sb = [sb("x_sb%d" % i, [128, D], F32) for i in range(NX)]
        junk = st.enter_context(nc.sbuf_tensor("junk", [128, D], BF16))
        for q_ in range(8):
            bufs["junk%d" % q_] = Buf("junk%d" % q_)
        JALL = [bufs["junk%d" % q_] for q_ in range(8)]
        ss = sb("ss", [128, 1], F32)
        rstd = sb("rstd", [128, 1], F32)
        h_bf = sb("h_bf", [128, D], BF16)
        hT = [sb("hT%d" % i, [128, 8, 128], BF16) for i in range(2)]
        qk_sb = sb("qk_sb", [128, 2, 512], F32)
        rt1 = sb("rt1", [128, 8, 64], F32)
        rt2 = sb("rt2", [128, 8, 64], F32)
        qr_bf = [sb("qr_bf%d" % i, [128, 2, 512], BF16) for i in range(2)]
        v_bf = [sb("v_bf%d" % i, [128, 1024], BF16) for i in range(2)]
        gate = [sb("gate%d" % i, [128, 1024], F32) for i in range(2)]
        sq_sb = sb("sq_sb", [128, 512], F32)
        th_sb = sb("th_sb", [128, 512], F32)
        kin_sb = sb("kin_sb", [128, 512], F32)
        logf_sb = sb("logf_sb", [128, 512], F32)
        lhi = sb("lhi", [128, 512], BF16)
        llo = sb("llo", [128, 512], BF16)
        eq_sb = th_sb
        bufs["eq_sb"] = bufs["th_sb"]
        hq_bf = [sb("hq_bf%d" % i, [128, 512], BF16) for i in range(2)]
        hk2_bf = sb("hk2_bf", [128, 512], BF16)
        hk_bf = [sb("hk_bf%d" % i, [128, 512], BF16) for i in range(2)]
        qT = [sb("qT%d" % i, [128, 8, 128], BF16) for i in range(2)]
        kT = [sb("kT%d" % i, [128, 8, 128], BF16) for i in range(2)]
        evec = [sb("evec%d" % i, [128, 2, 8], F32) for i in range(2)]
        for i_ in range(2):
            for g_ in range(2):
                for nm_ in ("gate", "v_bf", "qT", "kT", "evec"):
                    bufs["%s%d_%d" % (nm_, i_, g_)] = Buf("%s%d_%d" % (nm_, i_, g_))
        Am = [sb("Am%d" % i, [128, 4, 128], BF16) for i in range(2)]
        S_sb = sb("S_sb", [128, 8, 128], F32)
        bufs["S0"] = Buf("S0"); bufs["S1"] = Buf("S1")
        Sd_bf = sb("Sd_bf", [128, 8, 128], BF16)
        bufs["Sd0"] = Buf("Sd0"); bufs["Sd1"] = Buf("Sd1")
        ssq = sb("ssq", [128, 8], F32)
        rs = sb("rs", [128, 8], F32)
        for g_ in range(2):
            bufs["ssq%d" % g_] = Buf("ssq%d" % g_); bufs["rs%d" % g_] = Buf("rs%d" % g_)
        oT_sb = sb("oT_sb", [128, 8, 128], BF16)
        ss2 = sb("ss2", [128, 1], F32)
        rs2 = sb("rs2", [128, 1], F32)
        wst0_flat = wst[0][:].rearrange("p a b -> p (a b)")
        yr_sb = wst0_flat[:, 0:1024]
        yout = [wst0_flat[:, 1024:2048], wst0_flat[:, 2048:3072]]
        sqo_sb = wst0_flat[:, 3072:3584]
        oc1_t = sb("oc1_t", [128, 1024], BF16)
        oc_bf = [wst0_flat[:, 3584:4096].bitcast(BF16), oc1_t[:, :]]
        bufs["oc_bf1"] = bufs["oc1_t"]
        for n in ("yr_sb", "yout0", "yout1", "sqo_sb", "oc_bf0"):
            bufs[n] = Buf(n)
        vS = sqo_sb[0:NS, :].bitcast(BF16)
        kS = sqo_sb[32:32 + NS, :].bitcast(BF16)
        gSa = sqo_sb[64:64 + NS, :]
        gSb = sqo_sb[96:96 + NS, :]
        for n in ("vS", "kS", "gSa", "gSb"):
            bufs[n] = Buf(n)
        wst1_flat = wst[1][:].rearrange("p a b -> p (a b)")
        NCH = 8
        Ssc = [wst1_flat[:, c_ * 512:(c_ + 1) * 512].rearrange("p (t v) -> p t v", t=4) for c_ in range(NCH)]
        for c_ in range(NCH):
            for t_ in range(4):
                bufs["Ssc%d_%d" % (c_, t_)] = Buf("Ssc")
        decT = sb("decT", [128, 8, NS], F32)
        qTs = sb("qTs", [128, 8, NS], BF16)
        qmask = [sb("qmask%d" % i, [128, NS, NS], BF16) for i in range(2)]
        f_sb = logf_sb[0:NS, :]
        bufs["f_sb"] = bufs["logf_sb"]
        _gs = (n_tiles - 1) % 2
        Sbf = [gate[_gs][:, :].bitcast(BF16).rearrange("p (t d) -> p t d", t=NS),
               qk_sb[:, :, :].rearrange("p a b -> p (a b)").bitcast(BF16).rearrange("p (t d) -> p t d", t=NS)]
        bufs["Sbf0"] = Buf("Sbf0")
        bufs["Sbf1"] = bufs["qk_sb"]
        os_half = [sq_sb[0:NS, :].rearrange("p (h d) -> p h d", h=4), th_sb[0:NS, :].rearrange("p (h d) -> p h d", h=4)]
        _xs = (n_tiles - 1) % NX
        os_sb = x_sb[_xs][0:NS, :].rearrange("p (h d) -> p h d", h=8)
        bufs["os_sb"] = bufs["x_sb%d" % _xs]
        bufs["os_g0"] = Buf("os_g0"); bufs["os_g1"] = Buf("os_g1")
        _k0 = (n_tiles + 1) % NX; _k1 = (n_tiles + 2) % NX
        kmask = [x_sb[_k0][0:NS, :].bitcast(BF16).rearrange("p (t d) -> p t d", t=NS),
                 x_sb[_k1][0:NS, :].bitcast(BF16).rearrange("p (t d) -> p t d", t=NS)]
        bufs["kmask0"] = bufs["x_sb%d" % _k0]
        bufs["kmask1"] = bufs["x_sb%d" % _k1]

        pp = [ps("pp0", [128, 512], F32), ps("pp1", [128, 512], F32)]
        tp = ps("tp", [128, 8, 128], BF16)
        tTev = ps("tTev", [128, 512], F32)
        tT = tTev[:, 0:256].bitcast(BF16).rearrange("p (a b) -> p a b", a=4)
        ev_ps = tTev[:, 256:272].rearrange("p (h c) -> p h c", h=4)
        bufs["tT"] = bufs["tTev"]; bufs["ev_ps"] = bufs["tTev"]
        u_ps = ps("u_ps", [128, 512], F32)
        at_ps = ps("at_ps", [128, 4, 128], F32)
        o_ps0 = ps("o_ps0", [128, 4, 128], F32)
        misc = ps("misc", [128, 512], F32)
        o_ps = [o_ps0, misc[:, :].rearrange("p (a b) -> p a b", a=4)]
        bufs["o_ps1"] = bufs["misc"]
        os_ps = misc[:, 16:144]
        fT_ps = misc[:, 144:208].rearrange("p (h t) -> p h t", h=4)
        bufs["os_ps"] = bufs["misc"]; bufs["fT_ps"] = bufs["misc"]

        fac_v = cf[:, FAC_OFF:FAC_OFF + FAC_N].rearrange("p (w j) -> p w j", w=2)
        eret_v = cf[:, ERET_OFF:ERET_OFF + ERET_N].rearrange("p (c h) -> p c h", c=2)
        gam_v = cf[:, GAM_OFF:GAM_OFF + 4]
        mask_v = cf[:, MASK_OFF:MASK_OFF + 128]
        eye_v = cf[:, EYE_OFF:EYE_OFF + 256].rearrange("p (a b) -> p a b", a=16)
        idf_v = cf[:, IDF_OFF:IDF_OFF + 128]
        M1_bf = M12_bf[:, 0:128]
        M2_bf = M12_bf[:, 128:256]
        sel_bf = M12_bf[:, 256:260]

        stores = []
        NTOT = n_tiles + (1 if do_sample else 0)

        def tile_id(j):
            return NTILE if j == 0 else j - 1

        S.dma("sp", lambda e: [e.dma_start(out=cf[:], in_=cfd)], key="cf", w=[B("cf")], us=1.2)
        S.dma("sp", lambda e: [e.dma_start(out=ng_sb[:], in_=ng)], key="ng", w=[B("ng_sb")], us=0.3)
        S.dma("sp", lambda e: [e.dma_start(out=gn_sb[:], in_=gn)], key="gn", w=[B("gn_sb")], us=0.3)
        S.dma("sp", lambda e: [e.dma_start(out=sq_sb[:], in_=hglb[0:1, :].partition_broadcast(128))], key="lb0", w=[B("sq_sb")], us=0.3)
        S.dma("sp", lambda e: [e.dma_start(out=th_sb[:], in_=hglb[1:2, :].partition_broadcast(128))], key="lb1", w=[B("th_sb")], us=0.3)
        S.dma("sp", lambda e: [e.dma_start(out=fng_bc[:], in_=fng[0:1, :].partition_broadcast(128))], key="fng", w=[B("fng_bc")], us=0.3)

        S.dma("sp", lambda e: [e.dma_start(out=cb_sb[:], in_=cbd)], key="cb", w=[B("cb_sb")], us=0.3)
        S.op("pool", lambda e: e.memset(mhalf[:], -0.5), w=[B("mhalf")], c=0.5)
        for i in range(2):
            S.op("dve", lambda e, i=i: e.tensor_copy(out=evec[i][:, :, 0:4], in_=eret_v), r=[B("cf")], w=[B("evec%d_0" % i)], c=0.62)
            S.op("pool", lambda e, i=i: e.memset(Am[i][:], 0.0), w=[B("Am%d" % i)], c=0.5)
        S.op("pool", lambda e: e.memset(S_sb[:], 0.0), w=[B("S0"), B("S1")], c=0.5)
        S.op("dve", lambda e: e.tensor_tensor(out=negc1[:], in0=th_sb[:], in1=sq_sb[:], op=ALU.subtract),
             r=[B("sq_sb"), B("th_sb")], w=[B("negc1")], c=0.62)
        S.op("act", lambda e: e.activation(out=negc1[:], in_=negc1[:], func=AF.Tanh, scale=0.5), r=[B("negc1")], w=[B("negc1")], c=0.62, tab="silu")
        S.op("dve", lambda e: e.tensor_scalar(out=negc1[:], in0=negc1[:], scalar1=1.0, scalar2=-0.25, op0=ALU.add, op1=ALU.mult),
             r=[B("negc1")], w=[B("negc1")], c=0.62)

        RSQRT_ON_ACT = True

        def rsqrt_pool(dst, src, scale, P, cols, rb, wb):
            if RSQRT_ON_ACT:
                S.op("act", lambda e: e.activation(out=dst, in_=src, func=AF.Ln, scale=scale, bias=EPS), r=[rb], w=[wb], c=0.22, tab="lnexp")
                S.op("act", lambda e: e.activation(out=dst, in_=dst, func=AF.Exp, scale=-0.5), r=[wb], w=[wb], c=0.22, tab="lnexp")
            else:
                S.op("pool", lambda e: e.tensor_scalar(out=dst, in0=src, scalar1=scale, scalar2=EPS, op0=ALU.mult, op1=ALU.add), r=[rb], w=[wb], c=0.3)
                S.op("pool", lambda e: e.tensor_tensor(out=dst, in0=dst, in1=mhalf[P, cols], op=ALU.pow), r=[wb, B("mhalf")], w=[wb], c=0.3)

        wprep_cnt = [0]

        NWS = 16
        for n_ in range(NWS):
            bufs["wsl%d" % n_] = Buf("wsl")

        def prep_group(kind, g):
            if kind == "in":
                srcw = w_in[:, g * 512:(g + 1) * 512].rearrange("(kc p) n -> p kc n", p=128)
                dstbuf = B("win%d" % g)
                scl = ng_sb; sclb = B("ng_sb")
            else:
                srcw = w_out[:, g * 512:(g + 1) * 512].rearrange("(kc p) n -> p kc n", p=128)
                dstbuf = B("wout%d" % g)
                scl = gn_sb; sclb = B("gn_sb")
            engs = ["act", "dve", "pool", "act", "dve", "act", "dve", "pool"]
            for kc in range(8):
                n_ = wprep_cnt[0] % NWS
                wprep_cnt[0] += 1
                stg = wst[n_ // 8]
                k8 = n_ % 8
                sbuf_ = B("wsl%d" % n_)
                S.dma("sp", lambda e, stg=stg, k8=k8, kc=kc: [e.dma_start(out=stg[:, k8, :], in_=srcw[:, kc, :])],
                      key=("wsl", n_), w=[sbuf_], us=1.15)
                if kind == "in":
                    dst = w_in_bf[:, kc, g * 512:(g + 1) * 512]
                else:
                    dst = w_out_bf[:, kc, g * 512:(g + 1) * 512]
                en = engs[kc]
                srcv = stg[:, k8, :]
                if en == "act":
                    S.op("act", lambda e, dst=dst, kc=kc, srcv=srcv: e.activation(out=dst, in_=srcv, func=AF.Copy, scale=scl[:, kc:kc + 1]),
                         r=[sbuf_, sclb], w=[dstbuf], c=0.6)
                else:
                    S.op(en, lambda e, dst=dst, kc=kc, srcv=srcv: e.tensor_scalar(out=dst, in0=srcv, scalar1=scl[:, kc:kc + 1], scalar2=1.0,
                                                                                op0=ALU.mult, op1=ALU.mult),
                         r=[sbuf_, sclb], w=[dstbuf], c=(0.62 if en == "dve" else 1.27))

        G_ORDER = [5, 4, 3, 7, 0, 1, 2, 6]
        pp_ctr = [0]

        def next_pp():
            n = pp_ctr[0]; pp_ctr[0] += 1
            return pp[n % 2], B("pp%d" % (n % 2))

        def load_x(j):
            i = tile_id(j)
            NT = 128 if i < NTILE else NS
            sl = j % NX
            src = xp[i * 128:(i + 1) * 128, :] if i < NTILE else xs[:, :]
            S.dma("sp", lambda e: [e.dma_start(out=x_sb[sl][0:NT, :], in_=src)], key=("x", sl), w=[B("x_sb%d" % sl)])

        def load_cs(j):
            i = tile_id(j)
            sl = j % 2
            S.dma("sp", lambda e: [e.dma_start(out=cs_sb[sl][:].rearrange("p c f -> p (c f)"), in_=csd[i])], key=("cs", sl), w=[B("cs_sb%d" % sl)], us=0.3)

        def stage_A1(j):
            i = tile_id(j)
            NT = 128 if i < NTILE else NS
            xt = x_sb[j % NX]; xb = B("x_sb%d" % (j % NX))
            hTt = hT[j % 2]; hTb = B("hT%d" % (j % 2))
            P = slice(0, NT)
            S.op("act", lambda e: e.activation(out=junk[P, :], in_=xt[P, :], func=AF.Square, accum_out=ss[P, 0:1]), r=[xb], w=[B("ss")] + JALL, c=1.1)
            rsqrt_pool(rstd[P, :], ss[P, :], 1.0 / D, P, slice(0, 1), B("ss"), B("rstd"))
            S.op("pool", lambda e: e.tensor_scalar(out=h_bf[P, :], in0=xt[P, :], scalar1=rstd[P, 0:1], scalar2=1.0, op0=ALU.mult, op1=ALU.mult),
                 r=[xb, B("rstd")], w=[B("h_bf")], c=1.15)
            yield
            for kc in range(8):
                S.op("pe", lambda e, kc=kc: e.transpose(out=tp[:, kc, 0:NT], in_=h_bf[P, kc * 128:(kc + 1) * 128], identity=ident_bf[P, 0:NT]),
                     r=[B("h_bf"), B("ident_bf")], w=[B("tp")], c=0.055)
            S.op("dve", lambda e: e.tensor_copy(out=hTt[:, :, 0:NT], in_=tp[:, :, 0:NT]), r=[B("tp")], w=[hTb], c=0.7)
            yield

        def stage_A2(j):
            i = tile_id(j)
            NT = 128 if i < NTILE else NS
            sample = i >= NTILE
            sl = j % 2
            hTt = hT[j % 2]; hTb = B("hT%d" % (j % 2))
            cst = cs_sb[j % 2]; csb = B("cs_sb%d" % (j % 2))
            P = slice(0, NT)
            facw = 1 if sample else 0

            def proj(g):
                bank, bb = next_pp()
                for kc in range(8):
                    S.op("pe", lambda e, kc=kc: e.matmul(bank[P, :], lhsT=hTt[:, kc, 0:NT], rhs=w_in_bf[:, kc, g * 512:(g + 1) * 512],
                                                        start=(kc == 0), stop=(kc == 7)),
                         r=[hTb, B("win%d" % g)], w=[bb], c=0.22)
                return bank, bb

            bank, bb = proj(5)
            S.op("act", lambda e, bank=bank: e.activation(out=th_sb[P, :], in_=bank[P, :], func=AF.Tanh, scale=0.5), r=[bb], w=[B("th_sb")], c=0.62, tab="silu")
            S.op("dve", lambda e: e.scalar_tensor_tensor(out=kin_sb[P, :], in0=th_sb[P, :], scalar=1.0, in1=negc1[P, :], op0=ALU.subtract, op1=ALU.mult),
                 r=[B("th_sb"), B("negc1")], w=[B("kin_sb")], c=0.66)
            yield
            bank, bb = proj(4)
            S.op("act", lambda e, bank=bank: e.activation(out=sq_sb[P, :], in_=bank[P, :], func=AF.Silu), r=[bb], w=[B("sq_sb")], c=0.62, tab="silu")
            yield
            bank, bb = proj(3)
            S.op("act", lambda e, bank=bank: e.activation(out=gate[sl][P, 0:512], in_=bank[P, :], func=AF.Silu), r=[bb], w=[B("gate%d_0" % sl)], c=0.62, tab="silu")
            yield
            bank, bb = proj(7)
            S.op("act", lambda e, bank=bank: e.activation(out=gate[sl][P, 512:1024], in_=bank[P, :], func=AF.Silu), r=[bb], w=[B("gate%d_1" % sl)], c=0.62, tab="silu")
            if not sample:
                S.op("act", lambda e: e.activation(out=logf_sb[:], in_=kin_sb[:], func=AF.Ln, scale=-1.0, bias=1.0), r=[B("kin_sb")], w=[B("logf_sb")], c=0.62, tab="lnexp")
                S.op("dve", lambda e: e.tensor_copy(out=lhi[:], in_=logf_sb[:]), r=[B("logf_sb")], w=[B("lhi")], c=0.62)
                S.op("pool", lambda e: e.tensor_tensor(out=llo[:], in0=logf_sb[:], in1=lhi[:], op=ALU.subtract), r=[B("logf_sb"), B("lhi")], w=[B("llo")], c=1.27)
            else:
                S.op("dve", lambda e: e.tensor_copy(out=hq_bf[sl][P, :], in_=sq_sb[P, :]), r=[B("sq_sb")], w=[B("hq_bf%d" % sl)], c=0.62)
                S.op("dve", lambda e: e.tensor_copy(out=hk_bf[sl][P, :], in_=kin_sb[P, :]), r=[B("kin_sb")], w=[B("hk_bf%d" % sl)], c=0.62)
                S.op("dve", lambda e: e.tensor_scalar(out=f_sb, in0=kin_sb[P, :], scalar1=-1.0, scalar2=1.0, op0=ALU.mult, op1=ALU.add),
                     r=[B("kin_sb")], w=[B("f_sb")], c=0.62)
            yield
            bank, bb = proj(0)
            S.op("dve", lambda e, bank=bank: e.tensor_tensor(
                out=qk_sb[P, 0, :].rearrange("p (h d) -> p h d", h=4), in0=bank[P, :].rearrange("p (h d) -> p h d", h=4),
                in1=fac_v[P, facw, 0:4].unsqueeze(2).to_broadcast([NT, 4, 128]), op=ALU.mult), r=[bb, B("cf")], w=[B("qk_sb")], c=0.62)
            if not sample:
                S.op("pe", lambda e: e.matmul(u_ps[:], lhsT=M1_bf, rhs=lhi[:], start=True, stop=False), r=[B("M12_bf"), B("lhi")], w=[B("u_ps")], c=0.3)
                S.op("pe", lambda e: e.matmul(u_ps[:], lhsT=M1_bf, rhs=llo[:], start=False, stop=True), r=[B("M12_bf"), B("llo")], w=[B("u_ps")], c=0.3)
                for h in range(4):
                    S.op("pe", lambda e, h=h: e.matmul(ev_ps[:, h, :], lhsT=lhi[:, h * 128:(h + 1) * 128], rhs=sel_bf, start=True, stop=False),
                         r=[B("M12_bf"), B("lhi")], w=[B("ev_ps")], c=0.06)
                    S.op("pe", lambda e, h=h: e.matmul(ev_ps[:, h, :], lhsT=llo[:, h * 128:(h + 1) * 128], rhs=sel_bf, start=False, stop=True),
                         r=[B("M12_bf"), B("llo")], w=[B("ev_ps")], c=0.06)
                S.op("act", lambda e: e.activation(out=eq_sb[:], in_=u_ps[:], func=AF.Exp), r=[B("u_ps")], w=[B("eq_sb")], c=0.62, tab="lnexp")
                S.op("act", lambda e: e.activation(out=u_ps[:], in_=u_ps[:], func=AF.Exp, scale=-1.0), r=[B("u_ps")], w=[B("u_ps")], c=0.62, tab="lnexp")
                S.op("act", lambda e: e.activation(out=evec[sl][:, :, 4:8], in_=ev_ps[:, :, 0:2].rearrange("p h c -> p c h"), func=AF.Exp),
                     r=[B("ev_ps")], w=[B("evec%d_1" % sl)], c=0.15, tab="lnexp")
                S.op("dve", lambda e: e.tensor_tensor(out=hk2_bf[:], in0=u_ps[:], in1=kin_sb[:], op=ALU.mult), r=[B("u_ps"), B("kin_sb")], w=[B("hk2_bf")], c=0.62)
                S.op("pool", lambda e: e.tensor_tensor(out=hq_bf[sl][:], in0=eq_sb[:], in1=sq_sb[:], op=ALU.mult), r=[B("eq_sb"), B("sq_sb")], w=[B("hq_bf%d" % sl)], c=1.27)
            yield
            bank, bb = proj(1)
            S.op("dve", lambda e, bank=bank: e.tensor_tensor(
                out=qk_sb[P, 1, :].rearrange("p (h d) -> p h d", h=4), in0=bank[P, :].rearrange("p (h d) -> p h d", h=4),
                in1=fac_v[P, facw, 4:8].unsqueeze(2).to_broadcast([NT, 4, 128]), op=ALU.mult), r=[bb, B("cf")], w=[B("qk_sb")], c=0.62)
            if not sample:
                S.op("pe", lambda e: e.matmul(u_ps[:], lhsT=M2_bf, rhs=lhi[:], start=True, stop=False), r=[B("M12_bf"), B("lhi")], w=[B("u_ps")], c=0.3)
                S.op("pe", lambda e: e.matmul(u_ps[:], lhsT=M2_bf, rhs=llo[:], start=False, stop=True), r=[B("M12_bf"), B("llo")], w=[B("u_ps")], c=0.3)
                S.op("act", lambda e: e.activation(out=u_ps[:], in_=u_ps[:], func=AF.Exp), r=[B("u_ps")], w=[B("u_ps")], c=0.62, tab="lnexp")
                S.op("dve", lambda e: e.tensor_tensor(out=hk_bf[sl][:], in0=u_ps[:], in1=kin_sb[:], op=ALU.mult), r=[B("u_ps"), B("kin_sb")], w=[B("hk_bf%d" % sl)], c=0.62)
            qv = qk_sb[P, :, :].rearrange("p a (h t f) -> p (a h) t f", h=4, t=2)
            ov = qr_bf[sl][P, :, :].rearrange("p a (h t f) -> p (a h) t f", h=4, t=2)
            cosb = cst[P, 0, :].unsqueeze(1).to_broadcast([NT, 8, 64])
            sinb = cst[P, 1, :].unsqueeze(1).to_broadcast([NT, 8, 64])
            x1 = qv[:, :, 0, :]; x2 = qv[:, :, 1, :]
            rb = [B("qk_sb"), csb]
            S.op("pool", lambda e: e.tensor_tensor(out=rt1[P], in0=x1, in1=cosb, op=ALU.mult), r=rb, w=[B("rt1")], c=1.27)
            S.op("pool", lambda e: e.tensor_tensor(out=rt2[P], in0=x2, in1=sinb, op=ALU.mult), r=rb, w=[B("rt2")], c=1.27)
            S.op("pool", lambda e: e.tensor_tensor(out=ov[:, :, 0, :], in0=rt1[P], in1=rt2[P], op=ALU.subtract),
                 r=[B("rt1"), B("rt2")], w=[B("qr_bf%d" % sl)], c=1.27)
            S.op("pool", lambda e: e.tensor_tensor(out=rt1[P], in0=x1, in1=sinb, op=ALU.mult), r=rb, w=[B("rt1")], c=1.27)
            S.op("pool", lambda e: e.tensor_tensor(out=rt2[P], in0=x2, in1=cosb, op=ALU.mult), r=rb, w=[B("rt2")], c=1.27)
            S.op("pool", lambda e: e.tensor_tensor(out=ov[:, :, 1, :], in0=rt1[P], in1=rt2[P], op=ALU.add),
                 r=[B("rt1"), B("rt2")], w=[B("qr_bf%d" % sl)], c=1.27)
            yield
            def tround(srcs, sbn, dst, dstb):
                for jj in range(4):
                    S.op("pe", lambda e, jj=jj: e.transpose(out=tT[:, jj, :], in_=srcs[jj], identity=ident_bf),
                         r=[B(sbn), B("ident_bf")], w=[B("tT")], c=0.055)
                S.op("dve", lambda e: e.tensor_copy(out=dst, in_=tT), r=[B("tT")], w=[dstb], c=0.4)

            bank, bb = proj(2)
            S.op("act", lambda e, bank=bank: e.activation(out=v_bf[sl][P, 0:512], in_=bank[P, :], func=AF.Copy), r=[bb], w=[B("v_bf%d_0" % sl)], c=0.62)
            if not sample:
                tround([hq_bf[sl][:, jj * 128:(jj + 1) * 128] for jj in range(4)], "hq_bf%d" % sl, qT[sl][:, 4:8, :], B("qT%d_1" % sl))
            yield
            if not sample:
                tround([hk2_bf[:, jj * 128:(jj + 1) * 128] for jj in range(4)], "hk2_bf", kT[sl][:, 4:8, :], B("kT%d_1" % sl))
            yield
            bank, bb = proj(6)
            S.op("act", lambda e, bank=bank: e.activation(out=v_bf[sl][P, 512:1024], in_=bank[P, :], func=AF.Copy), r=[bb], w=[B("v_bf%d_1" % sl)], c=0.62)
            if not sample:
                tround([qr_bf[sl][:, 0, jj * 128:(jj + 1) * 128] for jj in range(4)], "qr_bf%d" % sl, qT[sl][:, 0:4, :], B("qT%d_0" % sl))
            yield
            if not sample:
                tround([qr_bf[sl][:, 1, jj * 128:(jj + 1) * 128] for jj in range(4)], "qr_bf%d" % sl, kT[sl][:, 0:4, :], B("kT%d_0" % sl))
            yield

        def norm_group(G, src, srcb, NT, gsl, ocs):
            P = slice(0, NT)
            hs = slice(4 * G, 4 * G + 4)
            for hl in range(4):
                h = 4 * G + hl
                S.op("act", lambda e, h=h, hl=hl: e.activation(out=junk[P, hl * 128:(hl + 1) * 128], in_=src[:, hl, :], func=AF.Square, accum_out=ssq[P, h:h + 1]),
                     r=[srcb], w=[B("ssq%d" % G), B("junk%d" % hl)], c=0.3)
            rsqrt_pool(rs[P, hs], ssq[P, hs], 1.0 / 128, P, hs, B("ssq%d" % G), B("rs%d" % G))
            for hl in range(4):
                h = 4 * G + hl
                S.op("dve", lambda e, h=h, hl=hl: e.scalar_tensor_tensor(
                    out=oc_bf[ocs][P, h * 128:(h + 1) * 128], in0=src[:, hl, :], scalar=rs[P, h:h + 1], in1=gate[gsl][P, h * 128:(h + 1) * 128],
                    op0=ALU.mult, op1=ALU.mult), r=[srcb, B("rs%d" % G), B("gate%d_%d" % (gsl, G))], w=[B("oc_bf%d" % ocs)], c=0.37)

        def tail_stage(j, force_sample=False, par=None, xsl=None, yr=None, yrb=None):
            i = NTILE if force_sample else tile_id(j)
            NT = 128 if i < NTILE else NS
            if yr is None:
                yr = yr_sb; yrb = B("yr_sb")
            P = slice(0, NT)
            ysl = (j % 2) if par is None else par
            ocs = ysl
            for kc in range(8):
                S.op("pe", lambda e, kc=kc: e.transpose(out=tp[:, kc, 0:NT], in_=oc_bf[ocs][P, kc * 128:(kc + 1) * 128], identity=ident_bf[P, 0:NT]),
                     r=[B("oc_bf%d" % ocs), B("ident_bf")], w=[B("tp")], c=0.055)
            S.op("act", lambda e: e.activation(out=oT_sb[:, :, 0:NT], in_=tp[:, :, 0:NT], func=AF.Copy), r=[B("tp")], w=[B("oT_sb")], c=1.1)
            yield
            xs_ = (j % NX) if xsl is None else xsl
            xt = x_sb[xs_]; xb = B("x_sb%d" % xs_)
            for g2 in range(2):
                bank, bb = next_pp()
                for kc in range(8):
                    S.op("pe", lambda e, kc=kc, g2=g2, bank=bank: e.matmul(bank[P, :], lhsT=oT_sb[:, kc, 0:NT], rhs=w_out_bf[:, kc, g2 * 512:(g2 + 1) * 512],
                                                                        start=(kc == 0), stop=(kc == 7)),
                         r=[B("oT_sb"), B("wout%d" % g2)], w=[bb], c=0.22)
                S.op("dve", lambda e, g2=g2, bank=bank: e.tensor_tensor(out=yr[P, g2 * 512:(g2 + 1) * 512], in0=bank[P, :], in1=xt[P, g2 * 512:(g2 + 1) * 512], op=ALU.add),
                     r=[bb, xb], w=[yrb], c=0.7)
                if g2 == 0:
                    yield
            S.op("act", lambda e: e.activation(out=junk[P, :], in_=yr[P, :], func=AF.Square, accum_out=ss2[P, 0:1]), r=[yrb], w=[B("ss2")] + JALL, c=1.1)
            rsqrt_pool(rs2[P, :], ss2[P, :], 1.0 / D, P, slice(0, 1), B("ss2"), B("rs2"))
            yo = yout[ysl]; yob = B("yout%d" % ysl)
            if i >= NTILE or j == NTOT - 1:
                S.op("dve", lambda e: e.scalar_tensor_tensor(out=yo[P, :], in0=yr[P, :], scalar=rs2[P, 0:1], in1=fng_bc[P, :],
                                                            op0=ALU.mult, op1=ALU.mult), r=[yrb, B("rs2"), B("fng_bc")], w=[yob], c=1.1)
            else:
                S.op("act", lambda e: e.activation(out=yo[P, :], in_=yr[P, :], func=AF.Copy, scale=rs2[P, 0:1]), r=[yrb, B("rs2")], w=[yob], c=1.25)
                for hh in range(2):
                    S.op("pool", lambda e, hh=hh: e.tensor_tensor(out=yo[P, hh * 512:(hh + 1) * 512], in0=yo[P, hh * 512:(hh + 1) * 512],
                                                                 in1=fng_bc[P, hh * 512:(hh + 1) * 512], op=ALU.mult), r=[yob, B("fng_bc")], w=[yob], c=1.27)
            dst = yp[i * 128:(i + 1) * 128, :] if i < NTILE else ys[:, :]
            stores.append(S.dma("sp", lambda e: [e.dma_start(out=dst, in_=yo[P, :])], key=("y", ysl), r=[yob]))
            yield

        def stage_B(j, last):
            sl = j % 2
            for G in range(2):
                qTb = B("qT%d_%d" % (sl, G)); kTb = B("kT%d_%d" % (sl, G))
                hs = slice(4 * G, 4 * G + 4)
                for hl in range(4):
                    h = 4 * G + hl
                    S.op("pe", lambda e, h=h, hl=hl: e.matmul(at_ps[0:64, hl, 0:64], lhsT=kT[sl][:, h, 0:64], rhs=qT[sl][:, h, 0:64], start=True, stop=True),
                         r=[qTb, kTb], w=[B("at_ps")], c=0.06)
                    S.op("pe", lambda e, h=h, hl=hl: e.matmul(at_ps[:, hl, 64:128], lhsT=kT[sl][:, h, :], rhs=qT[sl][:, h, 64:128], start=True, stop=True),
                         r=[qTb, kTb], w=[B("at_ps")], c=0.07)
                S.op("dve", lambda e, G=G: e.tensor_tensor(out=Am[G][0:64, :, 0:64], in0=at_ps[0:64, :, 0:64],
                                                          in1=mask_v[0:64, 0:64].unsqueeze(1).to_broadcast([64, 4, 64]), op=ALU.mult),
                     r=[B("at_ps"), B("cf")], w=[B("Am%d" % G)], c=0.35)
                S.op("dve", lambda e, G=G: e.tensor_tensor(out=Am[G][:, :, 64:128], in0=at_ps[:, :, 64:128],
                                                          in1=mask_v[:, 64:128].unsqueeze(1).to_broadcast([128, 4, 64]), op=ALU.mult),
                     r=[B("at_ps"), B("cf")], w=[B("Am%d" % G)], c=0.45)
                S.op("dve", lambda e, hs=hs: e.tensor_tensor(out=Sd_bf[:, hs, :], in0=S_sb[:, hs, :],
                                                            in1=evec[sl][:, 0, hs].unsqueeze(2).to_broadcast([128, 4, 128]), op=ALU.mult),
                     r=[B("S%d" % G), B("evec%d_%d" % (sl, G))], w=[B("Sd%d" % G)], c=0.62)
                yield
                for hl in range(4):
                    h = 4 * G + hl
                    S.op("pe", lambda e, h=h, hl=hl, G=G: e.matmul(o_ps[G][:, hl, :], lhsT=Am[G][:, hl, :], rhs=v_bf[sl][:, h * 128:(h + 1) * 128], start=True, stop=False),
                         r=[B("Am%d" % G), B("v_bf%d_%d" % (sl, G))], w=[B("o_ps%d" % G)], c=0.07)
                    S.op("pe", lambda e, h=h, hl=hl, G=G: e.matmul(o_ps[G][:, hl, :], lhsT=qT[sl][:, h, :], rhs=Sd_bf[:, h, :], start=False, stop=True),
                         r=[qTb, B("Sd%d" % G)], w=[B("o_ps%d" % G)], c=0.07)
                for hl in range(4):
                    h = 4 * G + hl
                    if G == 0:
                        ktok = qr_bf[sl][:, 1, hl * 128:(hl + 1) * 128]; kb = B("qr_bf%d" % sl)
                    else:
                        ktok = hk_bf[sl][:, hl * 128:(hl + 1) * 128]; kb = B("hk_bf%d" % sl)
                    S.op("pe", lambda e, h=h, hl=hl, ktok=ktok: e.matmul(at_ps[:, hl, :], lhsT=ktok, rhs=v_bf[sl][:, h * 128:(h + 1) * 128], start=True, stop=True),
                         r=[kb, B("v_bf%d_%d" % (sl, G))], w=[B("at_ps")], c=0.07)
                for hl in range(4):
                    h = 4 * G + hl
                    S.op("dve", lambda e, h=h, hl=hl: e.scalar_tensor_tensor(out=S_sb[:, h, :], in0=S_sb[:, h, :], scalar=evec[sl][:, 1, h:h + 1], in1=at_ps[:, hl, :],
                                                                            op0=ALU.mult, op1=ALU.add),
                         r=[B("S%d" % G), B("evec%d_%d" % (sl, G)), B("at_ps")], w=[B("S%d" % G)], c=0.37)
                if last:
                    dst = (nrp if G == 0 else nhp).rearrange("h d v -> d h v")
                    stores.append(S.dma("sp", lambda e, dst=dst, hs=hs: [e.dma_start(out=dst, in_=S_sb[:, hs, :])], key=("Sout", G), r=[B("S%d" % G)]))
                norm_group(G, o_ps[G], B("o_ps%d" % G), 128, sl, j % 2)
                yield

        def sample_pre():
            sl = 0
            P = slice(0, NS)
            for rnd in range(2):
                for jj in range(4):
                    h = rnd * 4 + jj
                    src = qr_bf[sl][P, 0, h * 128:(h + 1) * 128] if h < 4 else hq_bf[sl][P, (h - 4) * 128:(h - 3) * 128]
                    sbn = ("qr_bf%d" % sl) if h < 4 else ("hq_bf%d" % sl)
                    S.op("pe", lambda e, jj=jj, src=src: e.transpose(out=tT[:, jj, 0:NS], in_=src, identity=ident_bf[P, 0:NS]),
                         r=[B(sbn), B("ident_bf")], w=[B("tT")], c=0.055)
                S.op("dve", lambda e, rnd=rnd: e.tensor_copy(out=qTs[:, rnd * 4:(rnd + 1) * 4, :], in_=tT[:, :, 0:NS]), r=[B("tT")], w=[B("qTs")], c=0.3)
            S.op("dve", lambda e: e.tensor_copy(out=decT[:, 0:4, :], in_=gam_v.unsqueeze(2).to_broadcast([128, 4, NS])), r=[B("cf")], w=[B("decT")], c=0.2)
            for h in range(4):
                S.op("pe", lambda e, h=h: e.transpose(out=fT_ps[:, h, :], in_=f_sb[:, h * 128:(h + 1) * 128], identity=idf_v[P, 0:NS]),
                     r=[B("f_sb"), B("cf")], w=[B("fT_ps")], c=0.055)
            S.op("dve", lambda e: e.tensor_copy(out=decT[:, 4:8, :], in_=fT_ps), r=[B("fT_ps")], w=[B("decT")], c=0.2)
            S.op("act", lambda e: e.activation(out=vS, in_=v_bf[sl][P, :], func=AF.Copy), r=[B("v_bf%d_0" % sl), B("v_bf%d_1" % sl)], w=[B("vS")], c=1.0)
            S.op("pool", lambda e: e.tensor_copy(out=kS[:, 0:512], in_=qr_bf[sl][P, 1, :]), r=[B("qr_bf%d" % sl)], w=[B("kS")], c=0.8)
            S.op("pool", lambda e: e.tensor_copy(out=kS[:, 512:1024], in_=hk_bf[sl][P, :]), r=[B("hk_bf%d" % sl)], w=[B("kS")], c=0.8)
            S.op("act", lambda e: e.activation(out=gSa, in_=gate[sl][P, 0:512], func=AF.Copy), r=[B("gate%d_0" % sl)], w=[B("gSa")], c=0.6)
            S.op("act", lambda e: e.activation(out=gSb, in_=gate[sl][P, 512:1024], func=AF.Copy), r=[B("gate%d_1" % sl)], w=[B("gSb")], c=0.6)

        def sample_stage(j):
            sl = j % 2
            P = slice(0, NS)
            R32 = slice(32, 32 + NS)
            bufs["os_g0"] = bufs["sq_sb"]; bufs["os_g1"] = bufs["th_sb"]
            inherit(bufs["Sbf0"], ("gate%d_0" % _gs, "gate%d_1" % _gs))
            yield
            kvb = [(pp[0][:].rearrange("p (a b) -> p a b", a=4), B("pp0")), (pp[1][:].rearrange("p (a b) -> p a b", a=4), B("pp1")),
                   (u_ps[:].rearrange("p (a b) -> p a b", a=4), B("u_ps")), (o_ps0[:], B("o_ps0"))]

            LOOKAHEAD = 6

            def chunk_io(c):
                h = c // 4; q = c % 4
                src = (sret if h < 4 else shg)[q * 4:(q + 1) * 4, h % 4, :, :].rearrange("t d v -> d t v")
                dst = (nrs if h < 4 else nhs)[q * 4:(q + 1) * 4, h % 4, :, :].rearrange("t d v -> d t v")
                return src, dst

            def load_chunk(c):
                slot = c % NCH
                src, _ = chunk_io(c)
                tb = [B("Ssc%d_%d" % (slot, t)) for t in range(4)]
                S.dma("sp", lambda e: [e.dma_start(out=Ssc[slot], in_=src)], key=("Ssc", slot), w=tb)

            def head_pre(h):
                km = kmask[h % 2]; kmb = B("kmask%d" % (h % 2))
                ktok = kS[:, h * 128:(h + 1) * 128]; kb = B("kS")
                S.op("pool", lambda e: e.tensor_tensor(out=km, in0=ktok.unsqueeze(1).to_broadcast([NS, NS, 128]),
                                                      in1=idf_v[R32, 32:32 + NS].unsqueeze(2).to_broadcast([NS, NS, 128]), op=ALU.mult),
                     r=[kb, B("cf")], w=[kmb], c=3.6)
                qm = qmask[h % 2]; qmb = B("qmask%d" % (h % 2))
                S.op("pool", lambda e: e.tensor_tensor(out=qm[:], in0=qTs[:, h, :].unsqueeze(2).to_broadcast([128, NS, NS]),
                                                      in1=eye_v, op=ALU.mult),
                     r=[B("qTs"), B("cf")], w=[qmb], c=0.6)

            def do_chunk(c):
                h = c // 4; q = c % 4
                slot = c % NCH
                Sc = Ssc[slot]
                tb = [B("Ssc%d_%d" % (slot, t)) for t in range(4)]
                km = kmask[h % 2]; kmb = B("kmask%d" % (h % 2))
                bv, bkb = kvb[c % 4]
                _, dst = chunk_io(c)
                for tt in range(4):
                    t = q * 4 + tt
                    S.op("pe", lambda e, t=t, tt=tt: e.matmul(bv[:, tt, :], lhsT=km[:, t, :], rhs=vS[:, h * 128:(h + 1) * 128], start=True, stop=True),
                         r=[kmb, B("vS")], w=[bkb], c=0.07)
                for tt in range(4):
                    t = q * 4 + tt
                    S.op("dve", lambda e, t=t, tt=tt: e.scalar_tensor_tensor(out=Sc[:, tt, :], in0=Sc[:, tt, :], scalar=decT[:, h, t:t + 1], in1=bv[:, tt, :],
                                                                            op0=ALU.mult, op1=ALU.add),
                         r=[tb[tt], B("decT"), bkb], w=[tb[tt]], c=0.37)
                stores.append(S.dma("sp", lambda e: [e.dma_start(out=dst, in_=Sc)], key=("Ssc", slot), r=tb))
                Sf = Sbf[h % 2]; Sfb = B("Sbf%d" % (h % 2))
                S.op("act", lambda e: e.activation(out=Sf[:, q * 4:(q + 1) * 4, :], in_=Sc, func=AF.Copy), r=tb, w=[Sfb], c=0.6)

            def head_post(h):
                qm = qmask[h % 2]; qmb = B("qmask%d" % (h % 2))
                Sf = Sbf[h % 2]; Sfb = B("Sbf%d" % (h % 2))
                for t in range(NS):
                    S.op("pe", lambda e, t=t: e.matmul(os_ps[P, :], lhsT=qm[:, t, :], rhs=Sf[:, t, :], start=(t == 0), stop=(t == NS - 1)),
                         r=[qmb, Sfb], w=[B("os_ps")], c=0.07)
                S.op("act", lambda e: e.activation(out=os_half[h // 4][:, h % 4, :], in_=os_ps[P, :], func=AF.Copy), r=[B("os_ps")], w=[B("os_g%d" % (h // 4))], c=0.36)

            NCHUNK = 32
            for c in range(min(LOOKAHEAD, NCHUNK)):
                load_chunk(c)
            head_pre(0)
            for c in range(NCHUNK):
                h = c // 4; q = c % 4
                do_chunk(c)
                if c + LOOKAHEAD < NCHUNK:
                    load_chunk(c + LOOKAHEAD)
                if q == 1 and h >= 1:
                    head_post(h - 1)
                if q == 2 and h + 1 < 8:
                    head_pre(h + 1)
                if q == 3:
                    yield
            head_post(7)
            yield "chunks_done"
            slp = _gs
            S.op("act", lambda e: e.activation(out=gate[slp][P, 0:512], in_=gSa, func=AF.Copy), r=[B("gSa")], w=[B("gate%d_0" % slp), B("Sbf0")], c=0.6)
            S.op("act", lambda e: e.activation(out=gate[slp][P, 512:1024], in_=gSb, func=AF.Copy), r=[B("gSb")], w=[B("gate%d_1" % slp), B("Sbf0")], c=0.6)
            xsl_ = _k0
            S.dma("sp", lambda e: [e.dma_start(out=x_sb[xsl_][P, :], in_=xs[:, :])], key=("x", xsl_), w=[B("x_sb%d" % xsl_)], us=0.3)
            for G in range(2):
                norm_group(G, os_half[G], B("os_g%d" % G), NS, slp, slp)
            yr_s = qk_sb[:, :, :].rearrange("p a b -> p (a b)")
            for _ in tail_stage(j, force_sample=True, par=slp, xsl=xsl_, yr=yr_s, yrb=B("qk_sb")):
                yield

        def drain(g):
            for _ in g:
                pass

        def step(g):
            if g is None:
                return False
            try:
                next(g)
                return True
            except StopIteration:
                return False

        for j in range(min(3, NTOT)):
            load_x(j)
        load_cs(0)
        drain(stage_A1(0))
        if NTOT > 1:
            drain(stage_A1(1))
        for g in G_ORDER:
            prep_group("in", g)
        prep_group("out", 0)
        prep_group("out", 1)
        def inherit(buf, srcs):
            buf.last_w = None
            buf.readers = []
            for sname in srcs:
                sbf = B(sname)
                if sbf.last_w is not None:
                    buf.readers.append(sbf.last_w)
                buf.readers.extend(sbf.readers)

        for n in ("yr_sb", "yout0", "yout1", "sqo_sb", "oc_bf0", "vS", "kS", "gSa", "gSb"):
            inherit(bufs[n], tuple("wsl%d" % n_ for n_ in range(0, 8)))
        for c_ in range(NCH):
            for t_ in range(4):
                inherit(bufs["Ssc%d_%d" % (c_, t_)], tuple("wsl%d" % n_ for n_ in range(8, 16)))

        PATTERN = "sbscscs" + "bsbascsbsasbs"
        for it in range(-1, NTOT + 1):
            gA1 = stage_A1(it + 2) if it + 2 < NTOT else None
            gA2 = stage_A2(it + 1) if it + 1 < NTOT else None
            gB = stage_B(it, last=(it == NTOT - 1)) if 1 <= it < NTOT else None
            gC = tail_stage(it - 1) if 1 <= it - 1 < NTOT else None
            if it + 2 < NTOT:
                load_cs(it + 2)
            gens = {"a": gA1, "s": gA2, "b": gB, "c": gC}
            for ch in PATTERN:
                step(gens[ch])
            for g in (gC, gB, gA2, gA1):
                if g is not None:
                    drain(g)
            if it == -1:
                sample_pre()
            if it + 3 < NTOT:
                load_x(it + 3)
            if it == NTOT - 2:
                gS_ = sample_stage(n_tiles)
                for r_ in gS_:
                    if r_ == "chunks_done":
                        break
            if it == NTOT - 1:
                drain(gS_)

        if USE_LIST_SCHED:
            est = S.schedule()
            print("[sched] estimated us:", round(est, 1))
        with nc.Block() as block:
            S.emit(block, final_waits=stores)
    return nc


_CACHE = {}


def _get_program():
    if "nc" not in _CACHE:
        _CACHE["nc"] = build_program()
    return _CACHE["nc"]


def make_in_maps(x_prompt, x_sample, state_ret, state_hgrn, norm_g, w_in, ret_norm_g, hg_norm_g, hg_lb, w_out, final_norm_g, cores=range(NCORES)):
    f = lambda a: np.ascontiguousarray(np.asarray(a, dtype=np.float32))
    x_prompt = f(x_prompt); x_sample = f(x_sample); state_ret = f(state_ret); state_hgrn = f(state_hgrn)
    cf, cs, cb = _make_consts()
    ng = np.ascontiguousarray(f(norm_g)[0].reshape(8, 128).T)
    gn = np.ascontiguousarray(np.concatenate([f(ret_norm_g)[0], f(hg_norm_g)[0]]).reshape(8, 128).T)
    shared = {
        "cs": cs, "cb": cb,
        "w_in": f(w_in)[0], "w_out": f(w_out)[0], "ng": ng, "gn": gn, "hglb": f(hg_lb),
        "fng": f(final_norm_g).reshape(1, D), "cf": cf,
    }
    maps = []
    for c in cores:
        m = dict(shared)
        m["xp"] = x_prompt[c]
        m["xs"] = np.ascontiguousarray(x_sample[c * NS:(c + 1) * NS, 0, :])
        m["sret"] = np.ascontiguousarray(state_ret[0, c * NS:(c + 1) * NS])
        m["shg"] = np.ascontiguousarray(state_hgrn[0, c * NS:(c + 1) * NS])
        maps.append(m)
    return maps


def kernel(x_prompt, x_sample, state_ret, state_hgrn, norm_g, w_in, ret_norm_g, hg_norm_g, hg_lb, w_out, final_norm_g):
    nc = _get_program()
    maps = make_in_maps(x_prompt, x_sample, state_ret, state_hgrn, norm_g, w_in, ret_norm_g, hg_norm_g, hg_lb, w_out, final_norm_g)
    res = run_bass_kernel_spmd(nc, maps, core_ids=list(range(NCORES)))
    R = res.results
    y_prompt = np.stack([R[c]["yp"] for c in range(NCORES)], axis=0).astype(np.float32)
    y_sample = np.concatenate([R[c]["ys"] for c in range(NCORES)], axis=0).reshape(NCORES * NS, 1, D).astype(np.float32)
    nrp = np.stack([R[c]["nrp"] for c in range(NCORES)], axis=0)[None].astype(np.float32)
    nhp = np.stack([R[c]["nhp"] for c in range(NCORES)], axis=0)[None].astype(np.float32)
    nrs = np.concatenate([R[c]["nrs"] for c in range(NCORES)], axis=0)[None].astype(np.float32)
    nhs = np.concatenate([R[c]["nhs"] for c in range(NCORES)], axis=0)[None].astype(np.float32)
    return (y_prompt, y_sample, nrp, nhp, nrs, nhs)
```

```python
import numpy as np
import ml_dtypes
from contextlib import ExitStack

import concourse.bass as bass
import concourse.mybir as mybir
from concourse.bass_utils import run_bass_kernel_spmd

F32 = mybir.dt.float32
BF16 = mybir.dt.bfloat16
AF = mybir.ActivationFunctionType
ALU = mybir.AluOpType
AX = mybir.AxisListType

D = 1024
L = 2048
NTILE = 16
NS = 16
EPS = 1e-6
NCORES = 8
USE_LIST_SCHED = True

FAC_OFF = 0
FAC_N = 16
ERET_OFF = FAC_OFF + FAC_N
ERET_N = 8
GAM_OFF = ERET_OFF + ERET_N
GAM_N = 4
MASK_OFF = GAM_OFF + GAM_N
EYE_OFF = MASK_OFF + 128
IDF_OFF = EYE_OFF + 256
NCF = IDF_OFF + 128


def _make_consts():
    cf = np.zeros((128, NCF), np.float32)
    half = 64
    freqs = 1.0 / (10000.0 ** (np.arange(half, dtype=np.float64) / half))
    p = np.arange(128)
    cs = np.zeros((128, 17, 2, 64), np.float32)
    for i in range(17):
        pos = (i * 128 + p).astype(np.float64) if i < 16 else np.full(128, 16384.0, np.float64)
        ang = pos[:, None] * freqs[None, :]
        cs[:, i, 0, :] = np.cos(ang)
        cs[:, i, 1, :] = np.sin(ang)
    gam = 1.0 - np.exp2(-5.0 - np.arange(4, dtype=np.float64))
    t = p.astype(np.float64)
    fac = np.zeros((128, 2, 8), np.float64)
    for h in range(4):
        fac[:, 0, h] = gam[h] ** (t - 127.0)
        fac[:, 0, 4 + h] = (128.0 ** -0.5) * gam[h] ** (127.0 - t)
        fac[:, 1, h] = 1.0
        fac[:, 1, 4 + h] = 128.0 ** -0.5
    cf[:, FAC_OFF:FAC_OFF + FAC_N] = fac.reshape(128, -1)
    eret = np.zeros((128, 2, 4), np.float64)
    eret[:, 0, :] = gam[None, :] ** 128.0
    eret[:, 1, :] = gam[None, :] ** 128.0
    cf[:, ERET_OFF:ERET_OFF + ERET_N] = eret.reshape(128, -1)
    cf[:, GAM_OFF:GAM_OFF + 4] = gam[None, :]
    s = p[:, None]
    tt = p[None, :]
    cf[:, MASK_OFF:MASK_OFF + 128] = (s <= tt)
    cb = np.zeros((128, 128 + 260), np.float32)
    cb[:, 0:128] = np.eye(128, dtype=np.float32)
    m1 = np.zeros((128, 128), np.float32)
    m1[(s > 63) & (s <= tt)] = 1.0
    m1[(s > tt) & (s <= 63)] = -1.0
    cb[:, 128:256] = m1
    cb[:, 256:384] = (s > tt)
    cb[:, 384] = (p <= 63)
    cb[:, 385] = 1.0
    cf[:, EYE_OFF:EYE_OFF + 256] = np.eye(16, dtype=np.float32).reshape(1, 256)
    cf[:, IDF_OFF:IDF_OFF + 128] = np.eye(128, dtype=np.float32)
    cs_t = np.ascontiguousarray(cs.reshape(128, 17, 128).transpose(1, 0, 2))
    return cf, cs_t, cb.astype(ml_dtypes.bfloat16)


class Buf:
    __slots__ = ("name", "last_w", "readers")

    def __init__(self, name):
        self.name = name
        self.last_w = None
        self.readers = []


class Op:
    __slots__ = ("eng", "fn", "deps", "sem", "val", "need_inc", "is_dma", "ninc", "name", "alldeps", "cost", "done_lat", "tab", "idx")


DEFAULT_COST = {"pe": 0.24, "act": 0.72, "dve": 0.66, "pool": 1.3, "sp": 0.6}


class Sched:
    ENGS = ("pe", "act", "dve", "pool", "sp")
    ENGATTR = {"pe": "tensor", "act": "scalar", "dve": "vector", "pool": "gpsimd", "sp": "sync"}

    def __init__(self, nc, stack, same_engine_sync=True):
        self.nc = nc
        self.stack = stack
        self.ops = {e: [] for e in self.ENGS}
        self.eng_sem = {e: stack.enter_context(nc.semaphore("sem_" + e)) for e in self.ENGS}
        self.dma_sems = {}
        self.dma_cnt = {}
        self.same_engine_sync = same_engine_sync
        self.nops = 0

    def op(self, eng, fn, r=(), w=(), name="", c=None, tab=None):
        o = Op()
        o.eng = eng; o.fn = fn; o.is_dma = False; o.need_inc = False; o.name = name
        o.sem = None; o.val = None; o.ninc = 1
        o.cost = DEFAULT_COST[eng] if c is None else c
        o.done_lat = 0.0; o.tab = tab; o.idx = self.nops; self.nops += 1
        self._deps(o, r, w)
        self.ops[eng].append(o)
        return o

    def dma(self, eng, fn, key, r=(), w=(), n=1, name="", us=1.5):
        o = Op()
        o.eng = eng; o.fn = fn; o.is_dma = True; o.need_inc = True; o.name = name
        o.cost = 0.45 * n; o.done_lat = us; o.tab = None; o.idx = self.nops; self.nops += 1
        if key not in self.dma_sems:
            self.dma_sems[key] = self.stack.enter_context(self.nc.semaphore("dsem_%d" % len(self.dma_sems)))
            self.dma_cnt[key] = 0
        self.dma_cnt[key] += 16 * n
        o.sem = self.dma_sems[key]; o.val = self.dma_cnt[key]; o.ninc = n
        self._deps(o, r, w)
        self.ops[eng].append(o)
        return o

    def _deps(self, o, r, w):
        deps = []
        for b in r:
            if b.last_w is not None:
                deps.append(b.last_w)
        for b in w:
            if b.last_w is not None:
                deps.append(b.last_w)
            deps.extend(b.readers)
        seen = set(); out = []
        o.alldeps = []
        for d in deps:
            if id(d) in seen or d is o:
                continue
            seen.add(id(d))
            o.alldeps.append(d)
            if (not d.is_dma) and (not o.is_dma) and d.eng == o.eng and (o.eng == "pe" or not self.same_engine_sync):
                continue
            if not d.is_dma:
                d.need_inc = True
            out.append(d)
        o.deps = out
        for b in r:
            b.readers.append(o)
        for b in w:
            b.last_w = o
            b.readers = []

    def schedule(self, sem_lat=0.15, table_cost=1.3):
        import os as _os
        sem_lat = float(_os.environ.get("SCHED_SEMLAT", "1.0"))
        SAME_LAT = float(_os.environ.get("SCHED_SAMELAT", "0.02"))
        allops = []
        for e in self.ENGS:
            allops.extend(self.ops[e])
        allops.sort(key=lambda o: o.idx)
        ndeps = {}
        users = {}
        for o in allops:
            ndeps[id(o)] = len(o.alldeps)
            for d in o.alldeps:
                users.setdefault(id(d), []).append(o)
        fin = {}
        self.sim_start = {}
        self.sim_fin = fin
        blev = {}
        for o in reversed(allops):
            m = 0.0
            for u in users.get(id(o), ()):
                m = max(m, blev[id(u)] + (sem_lat if u.eng != o.eng else 0.0))
            blev[id(o)] = m + o.cost + (o.done_lat + 2.0 if o.is_dma else 0.0)
        EPS_T = float(_os.environ.get("SCHED_EPS", "0.0"))
        ready = {e: [] for e in self.ENGS}
        for o in allops:
            if ndeps[id(o)] == 0:
                ready[o.eng].append((0.0, o))
        free = {e: 0.0 for e in self.ENGS}
        cur_tab = [None]
        dma_free = [0.0]
        order = {e: [] for e in self.ENGS}
        remaining = len(allops)
        WINDOW = int(_os.environ.get("SCHED_WINDOW", "150"))
        sched = set()
        oldest = 0
        while remaining:
            while oldest < len(allops) and id(allops[oldest]) in sched:
                oldest += 1
            best = None
            for e in self.ENGS:
                for (rt, o) in ready[e]:
                    if o.idx > allops[oldest].idx + WINDOW:
                        continue
                    st = max(free[e], rt)
                    if e == "act" and o.tab is not None and o.tab != cur_tab[0]:
                        st += table_cost
                    key = ((round(st / EPS_T) if EPS_T > 0 else st), -blev[id(o)] if EPS_T > 0 else 0.0, o.idx)
                    if best is None or key < best[0]:
                        best = (key, e, rt, o)
            assert best is not None
            e, rt, o = best[1], best[2], best[3]
            st = max(free[e], rt)
            if e == "act" and o.tab is not None and o.tab != cur_tab[0]:
                st += table_cost
            ready[e].remove((rt, o))
            if e == "act" and o.tab is not None:
                cur_tab[0] = o.tab
            self.sim_start[id(o)] = st
            free[e] = st + o.cost
            if o.is_dma:
                tx0 = max(st + o.cost, dma_free[0])
                dma_free[0] = tx0 + o.done_lat
                fin[id(o)] = dma_free[0] + 2.0
            else:
                fin[id(o)] = st + o.cost
            order[e].append(o)
            sched.add(id(o))
            remaining -= 1
            for u in users.get(id(o), ()):
                ndeps[id(u)] -= 1
                if ndeps[id(u)] == 0:
                    rt_u = max(fin[id(d)] + (sem_lat if d.eng != u.eng else SAME_LAT) for d in u.alldeps)
                    ready[u.eng].append((rt_u, u))
        self.ops = order
        return max(free.values())

    def emit(self, block, final_waits=()):
        for e in self.ENGS:
            c = 0
            for o in self.ops[e]:
                if o.is_dma:
                    continue
                if o.need_inc:
                    c += 1
                    o.sem = self.eng_sem[e]; o.val = c
        finals = [(o.sem, o.val) for o in final_waits]
        for e in self.ENGS:
            ops = self.ops[e]

            def body(eng, ops=ops, e=e):
                known = {}
                for o in ops:
                    need = {}
                    for d in o.deps:
                        k = id(d.sem)
                        if known.get(k, 0) >= d.val:
                            continue
                        if k not in need or need[k][1] < d.val:
                            need[k] = (d.sem, d.val)
                    for k, (s, v) in need.items():
                        eng.wait_ge(s, v)
                        known[k] = v
                    res = o.fn(eng)
                    if o.is_dma:
                        assert len(res) == o.ninc, (o.name, len(res), o.ninc)
                        for ins in res:
                            ins.then_inc(o.sem, 16)
                    elif o.need_inc:
                        res.then_inc(o.sem, 1)
                if e == "sp":
                    best = {}
                    for (s, v) in finals:
                        if id(s) not in best or best[id(s)][1] < v:
                            best[id(s)] = (s, v)
                    for (s, v) in best.values():
                        eng.wait_ge(s, v)

            getattr(block, self.ENGATTR[e])(body)


def build_program(n_tiles=NTILE, do_sample=True):
    nc = bass.Bass("TRN2", target_bir_lowering=False)

    def din(name, shape, dt=F32):
        return nc.dram_tensor(name, shape, dt, kind="ExternalInput").ap()

    def dout(name, shape, dt=F32):
        return nc.dram_tensor(name, shape, dt, kind="ExternalOutput").ap()

    xp = din("xp", [L, D])
    xs = din("xs", [NS, D])
    sret = din("sret", [NS, 4, 128, 128])
    shg = din("shg", [NS, 4, 128, 128])
    w_in = din("w_in", [D, 4096])
    w_out = din("w_out", [D, D])
    ng = din("ng", [128, 8])
    gn = din("gn", [128, 8])
    hglb = din("hglb", [2, 512])
    fng = din("fng", [1, D])
    cfd = din("cf", [128, NCF])
    csd = din("cs", [17, 128, 128])
    cbd = din("cb", [128, 388], BF16)

    yp = dout("yp", [L, D])
    ys = dout("ys", [NS, D])
    nrp = dout("nrp", [4, 128, 128])
    nhp = dout("nhp", [4, 128, 128])
    nrs = dout("nrs", [NS, 4, 128, 128])
    nhs = dout("nhs", [NS, 4, 128, 128])

    with ExitStack() as st:
        S = Sched(nc, st)
        bufs = {}

        def sb(name, shape, dt=F32):
            t = st.enter_context(nc.sbuf_tensor(name, shape, dt))
            bufs[name] = Buf(name)
            return t

        def ps(name, shape, dt=F32):
            t = st.enter_context(nc.psum_tensor(name, shape, dt))
            bufs[name] = Buf(name)
            return t

        def B(name):
            return bufs[name]

        w_in_bf = sb("w_in_bf", [128, 8, 4096], BF16)
        for g in range(8):
            bufs["win%d" % g] = Buf("win%d" % g)
        w_out_bf = sb("w_out_bf", [128, 8, 1024], BF16)
        for g in range(2):
            bufs["wout%d" % g] = Buf("wout%d" % g)
        wst = [sb("wst%d" % i, [128, 8, 512], F32) for i in range(2)]
        cf = sb("cf_sb", [128, NCF], F32)
        bufs["cf"] = bufs["cf_sb"]
        cs_sb = [sb("cs_sb%d" % i, [128, 2, 64], F32) for i in range(2)]
        cb_sb = sb("cb_sb", [128, 388], BF16)
        bufs["ident_bf"] = bufs["cb_sb"]; bufs["M12_bf"] = bufs["cb_sb"]
        ident_bf = cb_sb[:, 0:128]
        M12_bf = cb_sb[:, 128:388]
        mhalf = sb("mhalf", [128, 8], F32)
        ng_sb = sb("ng_sb", [128, 8], F32)
        gn_sb = sb("gn_sb", [128, 8], F32)
        negc1 = sb("negc1", [128, 512], F32)
        fng_bc = sb("fng_bc", [128, D], F32)
        NX = 4
        x_sb = [sb("x_sb%d" % i, [128, D], F32) for i in range(NX)]
        junk = st.enter_context(nc.sbuf_tensor("junk", [128, D], BF16))
        for q_ in range(8):
            bufs["junk%d" % q_] = Buf("junk%d" % q_)
        JALL = [bufs["junk%d" % q_] for q_ in range(8)]
        ss = sb("ss", [128, 1], F32)
        rstd = sb("rstd", [128, 1], F32)
        h_bf = sb("h_bf", [128, D], BF16)
        hT = [sb("hT%d" % i, [128, 8, 128], BF16) for i in range(2)]
        qk_sb = sb("qk_sb", [128, 2, 512], F32)
        rt1 = sb("rt1", [128, 8, 64], F32)
        rt2 = sb("rt2", [128, 8, 64], F32)
        qr_bf = [sb("qr_bf%d" % i, [128, 2, 512], BF16) for i in range(2)]
        v_bf = [sb("v_bf%d" % i, [128, 1024], BF16) for i in range(2)]
        gate = [sb("gate%d" % i, [128, 1024], F32) for i in range(2)]
        sq_sb = sb("sq_sb", [128, 512], F32)
        th_sb = sb("th_sb", [128, 512], F32)
        kin_sb = sb("kin_sb", [128, 512], F32)
        logf_sb = sb("logf_sb", [128, 512], F32)
        lhi = sb("lhi", [128, 512], BF16)
        llo = sb("llo", [128, 512], BF16)
        eq_sb = th_sb
        bufs["eq_sb"] = bufs["th_sb"]
        hq_bf = [sb("hq_bf%d" % i, [128, 512], BF16) for i in range(2)]
        hk2_bf = sb("hk2_bf", [128, 512], BF16)
        hk_bf = [sb("hk_bf%d" % i, [128, 512], BF16) for i in range(2)]
        qT = [sb("qT%d" % i, [128, 8, 128], BF16) for i in range(2)]
        kT = [sb("kT%d" % i, [128, 8, 128], BF16) for i in range(2)]
        evec = [sb("evec%d" % i, [128, 2, 8], F32) for i in range(2)]
        for i_ in range(2):
            for g_ in range(2):
                bufs["evec%d_%d" % (i_, g_)] = Buf("evec%d_%d" % (i_, g_))
        Am = [sb("Am%d" % i, [128, 4, 128], BF16) for i in range(2)]
        S_sb = sb("S_sb", [128, 8, 128], F32)
        bufs["S0"] = Buf("S0"); bufs["S1"] = Buf("S1")
        Sd_bf = sb("Sd_bf", [128, 8, 128], BF16)
        bufs["Sd0"] = Buf("Sd0"); bufs["Sd1"] = Buf("Sd1")
        ssq = sb("ssq", [128, 8], F32)
        rs = sb("rs", [128, 8], F32)
        for g_ in range(2):
            bufs["ssq%d" % g_] = Buf("ssq%d" % g_); bufs["rs%d" % g_] = Buf("rs%d" % g_)
        oT_sb = sb("oT_sb", [128, 8, 128], BF16)
        ss2 = sb("ss2", [128, 1], F32)
        rs2 = sb("rs2", [128, 1], F32)
        wst0_flat = wst[0][:].rearrange("p a b -> p (a b)")
        yr_sb = wst0_flat[:, 0:1024]
        yout = [wst0_flat[:, 1024:2048], wst0_flat[:, 2048:3072]]
        sqo_sb = wst0_flat[:, 3072:3584]
        oc1_t = sb("oc1_t", [128, 1024], BF16)
        oc_bf = [wst0_flat[:, 3584:4096].bitcast(BF16), oc1_t[:, :]]
        bufs["oc_bf1"] = bufs["oc1_t"]
        for n in ("yr_sb", "yout0", "yout1", "sqo_sb", "oc_bf0"):
            bufs[n] = Buf(n)
        vS = sqo_sb[0:NS, :].bitcast(BF16)
        kS = sqo_sb[32:32 + NS, :].bitcast(BF16)
        gSa = sqo_sb[64:64 + NS, :]
        gSb = sqo_sb[96:96 + NS, :]
        for n in ("vS", "kS", "gSa", "gSb"):
            bufs[n] = Buf(n)
        wst1_flat = wst[1][:].rearrange("p a b -> p (a b)")
        NCH = 8
        Ssc = [wst1_flat[:, c_ * 512:(c_ + 1) * 512].rearrange("p (t v) -> p t v", t=4) for c_ in range(NCH)]
        for c_ in range(NCH):
            for t_ in range(4):
                bufs["Ssc%d_%d" % (c_, t_)] = Buf("Ssc")
        decT = sb("decT", [128, 8, NS], F32)
        qTs = sb("qTs", [128, 8, NS], BF16)
        qmask = [sb("qmask%d" % i, [128, NS, NS], BF16) for i in range(2)]
        f_sb = logf_sb[0:NS, :]
        bufs["f_sb"] = bufs["logf_sb"]
        _gs = (n_tiles - 1) % 2
        Sbf = [gate[_gs][:, :].bitcast(BF16).rearrange("p (t d) -> p t d", t=NS),
               qk_sb[:, :, :].rearrange("p a b -> p (a b)").bitcast(BF16).rearrange("p (t d) -> p t d", t=NS)]
        bufs["Sbf0"] = bufs["gate%d" % _gs]
        bufs["Sbf1"] = bufs["qk_sb"]
        os_half = [sq_sb[0:NS, :].rearrange("p (h d) -> p h d", h=4), th_sb[0:NS, :].rearrange("p (h d) -> p h d", h=4)]
        _xs = (n_tiles - 1) % NX
        os_sb = x_sb[_xs][0:NS, :].rearrange("p (h d) -> p h d", h=8)
        bufs["os_sb"] = bufs["x_sb%d" % _xs]
        bufs["os_g0"] = Buf("os_g0"); bufs["os_g1"] = Buf("os_g1")
        _k0 = (n_tiles + 1) % NX; _k1 = (n_tiles + 2) % NX
        kmask = [x_sb[_k0][0:NS, :].bitcast(BF16).rearrange("p (t d) -> p t d", t=NS),
                 x_sb[_k1][0:NS, :].bitcast(BF16).rearrange("p (t d) -> p t d", t=NS)]
        bufs["kmask0"] = bufs["x_sb%d" % _k0]
        bufs["kmask1"] = bufs["x_sb%d" % _k1]

        pp = [ps("pp0", [128, 512], F32), ps("pp1", [128, 512], F32)]
        tp = ps("tp", [128, 8, 128], BF16)
        tTev = ps("tTev", [128, 512], F32)
        tT = tTev[:, 0:256].bitcast(BF16).rearrange("p (a b) -> p a b", a=4)
        ev_ps = tTev[:, 256:272].rearrange("p (h c) -> p h c", h=4)
        bufs["tT"] = bufs["tTev"]; bufs["ev_ps"] = bufs["tTev"]
        u_ps = ps("u_ps", [128, 512], F32)
        at_ps = ps("at_ps", [128, 4, 128], F32)
        o_ps0 = ps("o_ps0", [128, 4, 128], F32)
        misc = ps("misc", [128, 512], F32)
        o_ps = [o_ps0, misc[:, :].rearrange("p (a b) -> p a b", a=4)]
        bufs["o_ps1"] = bufs["misc"]
        os_ps = misc[:, 16:144]
        fT_ps = misc[:, 144:208].rearrange("p (h t) -> p h t", h=4)
        bufs["os_ps"] = bufs["misc"]; bufs["fT_ps"] = bufs["misc"]

        fac_v = cf[:, FAC_OFF:FAC_OFF + FAC_N].rearrange("p (w j) -> p w j", w=2)
        eret_v = cf[:, ERET_OFF:ERET_OFF + ERET_N].rearrange("p (c h) -> p c h", c=2)
        gam_v = cf[:, GAM_OFF:GAM_OFF + 4]
        mask_v = cf[:, MASK_OFF:MASK_OFF + 128]
        eye_v = cf[:, EYE_OFF:EYE_OFF + 256].rearrange("p (a b) -> p a b", a=16)
        idf_v = cf[:, IDF_OFF:IDF_OFF + 128]
        M1_bf = M12_bf[:, 0:128]
        M2_bf = M12_bf[:, 128:256]
        sel_bf = M12_bf[:, 256:260]

        stores = []
        NTOT = n_tiles + (1 if do_sample else 0)

        def tile_id(j):
            return NTILE if j == 0 else j - 1

        S.dma("sp", lambda e: [e.dma_start(out=cf[:], in_=cfd)], key="cf", w=[B("cf")], us=1.2)
        S.dma("sp", lambda e: [e.dma_start(out=ng_sb[:], in_=ng)], key="ng", w=[B("ng_sb")], us=0.3)
        S.dma("sp", lambda e: [e.dma_start(out=gn_sb[:], in_=gn)], key="gn", w=[B("gn_sb")], us=0.3)
        S.dma("sp", lambda e: [e.dma_start(out=sq_sb[:], in_=hglb[0:1, :].partition_broadcast(128))], key="lb0", w=[B("sq_sb")], us=0.3)
        S.dma("sp", lambda e: [e.dma_start(out=th_sb[:], in_=hglb[1:2, :].partition_broadcast(128))], key="lb1", w=[B("th_sb")], us=0.3)
        S.dma("sp", lambda e: [e.dma_start(out=fng_bc[:], in_=fng[0:1, :].partition_broadcast(128))], key="fng", w=[B("fng_bc")], us=0.3)

        S.dma("sp", lambda e: [e.dma_start(out=cb_sb[:], in_=cbd)], key="cb", w=[B("cb_sb")], us=0.3)
        S.op("pool", lambda e: e.memset(mhalf[:], -0.5), w=[B("mhalf")], c=0.5)
        for i in range(2):
            S.op("dve", lambda e, i=i: e.tensor_copy(out=evec[i][:, :, 0:4], in_=eret_v), r=[B("cf")], w=[B("evec%d_0" % i)], c=0.62)
            S.op("pool", lambda e, i=i: e.memset(Am[i][:], 0.0), w=[B("Am%d" % i)], c=0.5)
        S.op("pool", lambda e: e.memset(S_sb[:], 0.0), w=[B("S0"), B("S1")], c=0.5)
        S.op("dve", lambda e: e.tensor_tensor(out=negc1[:], in0=th_sb[:], in1=sq_sb[:], op=ALU.subtract),
             r=[B("sq_sb"), B("th_sb")], w=[B("negc1")], c=0.62)
        S.op("act", lambda e: e.activation(out=negc1[:], in_=negc1[:], func=AF.Tanh, scale=0.5), r=[B("negc1")], w=[B("negc1")], c=0.62, tab="silu")
        S.op("dve", lambda e: e.tensor_scalar(out=negc1[:], in0=negc1[:], scalar1=1.0, scalar2=-0.25, op0=ALU.add, op1=ALU.mult),
             r=[B("negc1")], w=[B("negc1")], c=0.62)

        RSQRT_ON_ACT = True

        def rsqrt_pool(dst, src, scale, P, cols, rb, wb):
            if RSQRT_ON_ACT:
                S.op("act", lambda e: e.activation(out=dst, in_=src, func=AF.Ln, scale=scale, bias=EPS), r=[rb], w=[wb], c=0.22, tab="lnexp")
                S.op("act", lambda e: e.activation(out=dst, in_=dst, func=AF.Exp, scale=-0.5), r=[wb], w=[wb], c=0.22, tab="lnexp")
            else:
                S.op("pool", lambda e: e.tensor_scalar(out=dst, in0=src, scalar1=scale, scalar2=EPS, op0=ALU.mult, op1=ALU.add), r=[rb], w=[wb], c=0.3)
                S.op("pool", lambda e: e.tensor_tensor(out=dst, in0=dst, in1=mhalf[P, cols], op=ALU.pow), r=[wb, B("mhalf")], w=[wb], c=0.3)

        wprep_cnt = [0]

        NWS = 16
        for n_ in range(NWS):
            bufs["wsl%d" % n_] = Buf("wsl")

        def prep_group(kind, g):
            if kind == "in":
                srcw = w_in[:, g * 512:(g + 1) * 512].rearrange("(kc p) n -> p kc n", p=128)
                dstbuf = B("win%d" % g)
                scl = ng_sb; sclb = B("ng_sb")
            else:
                srcw = w_out[:, g * 512:(g + 1) * 512].rearrange("(kc p) n -> p kc n", p=128)
                dstbuf = B("wout%d" % g)
                scl = gn_sb; sclb = B("gn_sb")
            engs = ["act", "dve", "pool", "act", "dve", "act", "dve", "pool"]
            for kc in range(8):
                n_ = wprep_cnt[0] % NWS
                wprep_cnt[0] += 1
                stg = wst[n_ // 8]
                k8 = n_ % 8
                sbuf_ = B("wsl%d" % n_)
                S.dma("sp", lambda e, stg=stg, k8=k8, kc=kc: [e.dma_start(out=stg[:, k8, :], in_=srcw[:, kc, :])],
                      key=("wsl", n_), w=[sbuf_], us=1.15)
                if kind == "in":
                    dst = w_in_bf[:, kc, g * 512:(g + 1) * 512]
                else:
                    dst = w_out_bf[:, kc, g * 512:(g + 1) * 512]
                en = engs[kc]
                srcv = stg[:, k8, :]
                if en == "act":
                    S.op("act", lambda e, dst=dst, kc=kc, srcv=srcv: e.activation(out=dst, in_=srcv, func=AF.Copy, scale=scl[:, kc:kc + 1]),
                         r=[sbuf_, sclb], w=[dstbuf], c=0.6)
                else:
                    S.op(en, lambda e, dst=dst, kc=kc, srcv=srcv: e.tensor_scalar(out=dst, in0=srcv, scalar1=scl[:, kc:kc + 1], scalar2=1.0,
                                                                                op0=ALU.mult, op1=ALU.mult),
                         r=[sbuf_, sclb], w=[dstbuf], c=(0.62 if en == "dve" else 1.27))

        G_ORDER = [5, 4, 3, 7, 0, 1, 2, 6]
        pp_ctr = [0]

        def next_pp():
            n = pp_ctr[0]; pp_ctr[0] += 1
            return pp[n % 2], B("pp%d" % (n % 2))

        def load_x(j):
            i = tile_id(j)
            NT = 128 if i < NTILE else NS
            sl = j % NX
            src = xp[i * 128:(i + 1) * 128, :] if i < NTILE else xs[:, :]
            S.dma("sp", lambda e: [e.dma_start(out=x_sb[sl][0:NT, :], in_=src)], key=("x", sl), w=[B("x_sb%d" % sl)])

        def load_cs(j):
            i = tile_id(j)
            sl = j % 2
            S.dma("sp", lambda e: [e.dma_start(out=cs_sb[sl][:].rearrange("p c f -> p (c f)"), in_=csd[i])], key=("cs", sl), w=[B("cs_sb%d" % sl)], us=0.3)

        def stage_A1(j):
            i = tile_id(j)
            NT = 128 if i < NTILE else NS
            xt = x_sb[j % NX]; xb = B("x_sb%d" % (j % NX))
            hTt = hT[j % 2]; hTb = B("hT%d" % (j % 2))
            P = slice(0, NT)
            S.op("act", lambda e: e.activation(out=junk[P, :], in_=xt[P, :], func=AF.Square, accum_out=ss[P, 0:1]), r=[xb], w=[B("ss")] + JALL, c=1.1)
            rsqrt_pool(rstd[P, :], ss[P, :], 1.0 / D, P, slice(0, 1), B("ss"), B("rstd"))
            S.op("pool", lambda e: e.tensor_scalar(out=h_bf[P, :], in0=xt[P, :], scalar1=rstd[P, 0:1], scalar2=1.0, op0=ALU.mult, op1=ALU.mult),
                 r=[xb, B("rstd")], w=[B("h_bf")], c=1.15)
            yield
            for kc in range(8):
                S.op("pe", lambda e, kc=kc: e.transpose(out=tp[:, kc, 0:NT], in_=h_bf[P, kc * 128:(kc + 1) * 128], identity=ident_bf[P, 0:NT]),
                     r=[B("h_bf"), B("ident_bf")], w=[B("tp")], c=0.055)
            S.op("dve", lambda e: e.tensor_copy(out=hTt[:, :, 0:NT], in_=tp[:, :, 0:NT]), r=[B("tp")], w=[hTb], c=0.7)
            yield

        def stage_A2(j):
            i = tile_id(j)
            NT = 128 if i < NTILE else NS
            sample = i >= NTILE
            sl = j % 2
            hTt = hT[j % 2]; hTb = B("hT%d" % (j % 2))
            cst = cs_sb[j % 2]; csb = B("cs_sb%d" % (j % 2))
            P = slice(0, NT)
            facw = 1 if sample else 0

            def proj(g):
                bank, bb = next_pp()
                for kc in range(8):
                    S.op("pe", lambda e, kc=kc: e.matmul(bank[P, :], lhsT=hTt[:, kc, 0:NT], rhs=w_in_bf[:, kc, g * 512:(g + 1) * 512],
                                                        start=(kc == 0), stop=(kc == 7)),
                         r=[hTb, B("win%d" % g)], w=[bb], c=0.22)
                return bank, bb

            bank, bb = proj(5)
            S.op("act", lambda e, bank=bank: e.activation(out=th_sb[P, :], in_=bank[P, :], func=AF.Tanh, scale=0.5), r=[bb], w=[B("th_sb")], c=0.62, tab="silu")
            S.op("dve", lambda e: e.scalar_tensor_tensor(out=kin_sb[P, :], in0=th_sb[P, :], scalar=1.0, in1=negc1[P, :], op0=ALU.subtract, op1=ALU.mult),
                 r=[B("th_sb"), B("negc1")], w=[B("kin_sb")], c=0.66)
            yield
            bank, bb = proj(4)
            S.op("act", lambda e, bank=bank: e.activation(out=sq_sb[P, :], in_=bank[P, :], func=AF.Silu), r=[bb], w=[B("sq_sb")], c=0.62, tab="silu")
            yield
            bank, bb = proj(3)
            S.op("act", lambda e, bank=bank: e.activation(out=gate[sl][P, 0:512], in_=bank[P, :], func=AF.Silu), r=[bb], w=[B("gate%d" % sl)], c=0.62, tab="silu")
            yield
            bank, bb = proj(7)
            S.op("act", lambda e, bank=bank: e.activation(out=gate[sl][P, 512:1024], in_=bank[P, :], func=AF.Silu), r=[bb], w=[B("gate%d" % sl)], c=0.62, tab="silu")
            if not sample:
                S.op("act", lambda e: e.activation(out=logf_sb[:], in_=kin_sb[:], func=AF.Ln, scale=-1.0, bias=1.0), r=[B("kin_sb")], w=[B("logf_sb")], c=0.62, tab="lnexp")
                S.op("dve", lambda e: e.tensor_copy(out=lhi[:], in_=logf_sb[:]), r=[B("logf_sb")], w=[B("lhi")], c=0.62)
                S.op("pool", lambda e: e.tensor_tensor(out=llo[:], in0=logf_sb[:], in1=lhi[:], op=ALU.subtract), r=[B("logf_sb"), B("lhi")], w=[B("llo")], c=1.27)
            else:
                S.op("dve", lambda e: e.tensor_copy(out=hq_bf[sl][P, :], in_=sq_sb[P, :]), r=[B("sq_sb")], w=[B("hq_bf%d" % sl)], c=0.62)
                S.op("dve", lambda e: e.tensor_copy(out=hk_bf[sl][P, :], in_=kin_sb[P, :]), r=[B("kin_sb")], w=[B("hk_bf%d" % sl)], c=0.62)
                S.op("dve", lambda e: e.tensor_scalar(out=f_sb, in0=kin_sb[P, :], scalar1=-1.0, scalar2=1.0, op0=ALU.mult, op1=ALU.add),
                     r=[B("kin_sb")], w=[B("f_sb")], c=0.62)
            yield
            bank, bb = proj(0)
            S.op("dve", lambda e, bank=bank: e.tensor_tensor(
                out=qk_sb[P, 0, :].rearrange("p (h d) -> p h d", h=4), in0=bank[P, :].rearrange("p (h d) -> p h d", h=4),
                in1=fac_v[P, facw, 0:4].unsqueeze(2).to_broadcast([NT, 4, 128]), op=ALU.mult), r=[bb, B("cf")], w=[B("qk_sb")], c=0.62)
            if not sample:
                S.op("pe", lambda e: e.matmul(u_ps[:], lhsT=M1_bf, rhs=lhi[:], start=True, stop=False), r=[B("M12_bf"), B("lhi")], w=[B("u_ps")], c=0.3)
                S.op("pe", lambda e: e.matmul(u_ps[:], lhsT=M1_bf, rhs=llo[:], start=False, stop=True), r=[B("M12_bf"), B("llo")], w=[B("u_ps")], c=0.3)
                for h in range(4):
                    S.op("pe", lambda e, h=h: e.matmul(ev_ps[:, h, :], lhsT=lhi[:, h * 128:(h + 1) * 128], rhs=sel_bf, start=True, stop=False),
                         r=[B("M12_bf"), B("lhi")], w=[B("ev_ps")], c=0.06)
                    S.op("pe", lambda e, h=h: e.matmul(ev_ps[:, h, :], lhsT=llo[:, h * 128:(h + 1) * 128], rhs=sel_bf, start=False, stop=True),
                         r=[B("M12_bf"), B("llo")], w=[B("ev_ps")], c=0.06)
                S.op("act", lambda e: e.activation(out=eq_sb[:], in_=u_ps[:], func=AF.Exp), r=[B("u_ps")], w=[B("eq_sb")], c=0.62, tab="lnexp")
                S.op("act", lambda e: e.activation(out=u_ps[:], in_=u_ps[:], func=AF.Exp, scale=-1.0), r=[B("u_ps")], w=[B("u_ps")], c=0.62, tab="lnexp")
                S.op("act", lambda e: e.activation(out=evec[sl][:, :, 4:8], in_=ev_ps[:, :, 0:2].rearrange("p h c -> p c h"), func=AF.Exp),
                     r=[B("ev_ps")], w=[B("evec%d_1" % sl)], c=0.15, tab="lnexp")
                S.op("dve", lambda e: e.tensor_tensor(out=hk2_bf[:], in0=u_ps[:], in1=kin_sb[:], op=ALU.mult), r=[B("u_ps"), B("kin_sb")], w=[B("hk2_bf")], c=0.62)
                S.op("pool", lambda e: e.tensor_tensor(out=hq_bf[sl][:], in0=eq_sb[:], in1=sq_sb[:], op=ALU.mult), r=[B("eq_sb"), B("sq_sb")], w=[B("hq_bf%d" % sl)], c=1.27)
            yield
            bank, bb = proj(1)
            S.op("dve", lambda e, bank=bank: e.tensor_tensor(
                out=qk_sb[P, 1, :].rearrange("p (h d) -> p h d", h=4), in0=bank[P, :].rearrange("p (h d) -> p h d", h=4),
                in1=fac_v[P, facw, 4:8].unsqueeze(2).to_broadcast([NT, 4, 128]), op=ALU.mult), r=[bb, B("cf")], w=[B("qk_sb")], c=0.62)
            if not sample:
                S.op("pe", lambda e: e.matmul(u_ps[:], lhsT=M2_bf, rhs=lhi[:], start=True, stop=False), r=[B("M12_bf"), B("lhi")], w=[B("u_ps")], c=0.3)
                S.op("pe", lambda e: e.matmul(u_ps[:], lhsT=M2_bf, rhs=llo[:], start=False, stop=True), r=[B("M12_bf"), B("llo")], w=[B("u_ps")], c=0.3)
                S.op("act", lambda e: e.activation(out=u_ps[:], in_=u_ps[:], func=AF.Exp), r=[B("u_ps")], w=[B("u_ps")], c=0.62, tab="lnexp")
                S.op("dve", lambda e: e.tensor_tensor(out=hk_bf[sl][:], in0=u_ps[:], in1=kin_sb[:], op=ALU.mult), r=[B("u_ps"), B("kin_sb")], w=[B("hk_bf%d" % sl)], c=0.62)
            qv = qk_sb[P, :, :].rearrange("p a (h t f) -> p (a h) t f", h=4, t=2)
            ov = qr_bf[sl][P, :, :].rearrange("p a (h t f) -> p (a h) t f", h=4, t=2)
            cosb = cst[P, 0, :].unsqueeze(1).to_broadcast([NT, 8, 64])
            sinb = cst[P, 1, :].unsqueeze(1).to_broadcast([NT, 8, 64])
            x1 = qv[:, :, 0, :]; x2 = qv[:, :, 1, :]
            rb = [B("qk_sb"), csb]
            S.op("pool", lambda e: e.tensor_tensor(out=rt1[P], in0=x1, in1=cosb, op=ALU.mult), r=rb, w=[B("rt1")], c=1.27)
            S.op("pool", lambda e: e.tensor_tensor(out=rt2[P], in0=x2, in1=sinb, op=ALU.mult), r=rb, w=[B("rt2")], c=1.27)
            S.op("pool", lambda e: e.tensor_tensor(out=ov[:, :, 0, :], in0=rt1[P], in1=rt2[P], op=ALU.subtract),
                 r=[B("rt1"), B("rt2")], w=[B("qr_bf%d" % sl)], c=1.27)
            S.op("pool", lambda e: e.tensor_tensor(out=rt1[P], in0=x1, in1=sinb, op=ALU.mult), r=rb, w=[B("rt1")], c=1.27)
            S.op("pool", lambda e: e.tensor_tensor(out=rt2[P], in0=x2, in1=cosb, op=ALU.mult), r=rb, w=[B("rt2")], c=1.27)
            S.op("pool", lambda e: e.tensor_tensor(out=ov[:, :, 1, :], in0=rt1[P], in1=rt2[P], op=ALU.add),
                 r=[B("rt1"), B("rt2")], w=[B("qr_bf%d" % sl)], c=1.27)
            yield
            def tround(srcs, sbn, dst, dstb):
                for jj in range(4):
                    S.op("pe", lambda e, jj=jj: e.transpose(out=tT[:, jj, :], in_=srcs[jj], identity=ident_bf),
                         r=[B(sbn), B("ident_bf")], w=[B("tT")], c=0.055)
                S.op("dve", lambda e: e.tensor_copy(out=dst, in_=tT), r=[B("tT")], w=[dstb], c=0.4)

            bank, bb = proj(2)
            S.op("act", lambda e, bank=bank: e.activation(out=v_bf[sl][P, 0:512], in_=bank[P, :], func=AF.Copy), r=[bb], w=[B("v_bf%d" % sl)], c=0.62)
            if not sample:
                tround([hq_bf[sl][:, jj * 128:(jj + 1) * 128] for jj in range(4)], "hq_bf%d" % sl, qT[sl][:, 4:8, :], B("qT%d" % sl))
            yield
            if not sample:
                tround([hk2_bf[:, jj * 128:(jj + 1) * 128] for jj in range(4)], "hk2_bf", kT[sl][:, 4:8, :], B("kT%d" % sl))
            yield
            bank, bb = proj(6)
            S.op("act", lambda e, bank=bank: e.activation(out=v_bf[sl][P, 512:1024], in_=bank[P, :], func=AF.Copy), r=[bb], w=[B("v_bf%d" % sl)], c=0.62)
            if not sample:
                tround([qr_bf[sl][:, 0, jj * 128:(jj + 1) * 128] for jj in range(4)], "qr_bf%d" % sl, qT[sl][:, 0:4, :], B("qT%d" % sl))
            yield
            if not sample:
                tround([qr_bf[sl][:, 1, jj * 128:(jj + 1) * 128] for jj in range(4)], "qr_bf%d" % sl, kT[sl][:, 0:4, :], B("kT%d" % sl))
            yield

        def norm_group(G, src, srcb, NT, gsl, ocs):
            P = slice(0, NT)
            hs = slice(4 * G, 4 * G + 4)
            for hl in range(4):
                h = 4 * G + hl
                S.op("act", lambda e, h=h, hl=hl: e.activation(out=junk[P, hl * 128:(hl + 1) * 128], in_=src[:, hl, :], func=AF.Square, accum_out=ssq[P, h:h + 1]),
                     r=[srcb], w=[B("ssq%d" % G), B("junk%d" % hl)], c=0.3)
            rsqrt_pool(rs[P, hs], ssq[P, hs], 1.0 / 128, P, hs, B("ssq%d" % G), B("rs%d" % G))
            for hl in range(4):
                h = 4 * G + hl
                S.op("dve", lambda e, h=h, hl=hl: e.scalar_tensor_tensor(
                    out=oc_bf[ocs][P, h * 128:(h + 1) * 128], in0=src[:, hl, :], scalar=rs[P, h:h + 1], in1=gate[gsl][P, h * 128:(h + 1) * 128],
                    op0=ALU.mult, op1=ALU.mult), r=[srcb, B("rs%d" % G), B("gate%d" % gsl)], w=[B("oc_bf%d" % ocs)], c=0.37)

        def tail_stage(j, force_sample=False, par=None, xsl=None, yr=None, yrb=None):
            i = NTILE if force_sample else tile_id(j)
            NT = 128 if i < NTILE else NS
            if yr is None:
                yr = yr_sb; yrb = B("yr_sb")
            P = slice(0, NT)
            ysl = (j % 2) if par is None else par
            ocs = ysl
            for kc in range(8):
                S.op("pe", lambda e, kc=kc: e.transpose(out=tp[:, kc, 0:NT], in_=oc_bf[ocs][P, kc * 128:(kc + 1) * 128], identity=ident_bf[P, 0:NT]),
                     r=[B("oc_bf%d" % ocs), B("ident_bf")], w=[B("tp")], c=0.055)
            S.op("act", lambda e: e.activation(out=oT_sb[:, :, 0:NT], in_=tp[:, :, 0:NT], func=AF.Copy), r=[B("tp")], w=[B("oT_sb")], c=1.1)
            yield
            xs_ = (j % NX) if xsl is None else xsl
            xt = x_sb[xs_]; xb = B("x_sb%d" % xs_)
            for g2 in range(2):
                bank, bb = next_pp()
                for kc in range(8):
                    S.op("pe", lambda e, kc=kc, g2=g2, bank=bank: e.matmul(bank[P, :], lhsT=oT_sb[:, kc, 0:NT], rhs=w_out_bf[:, kc, g2 * 512:(g2 + 1) * 512],
                                                                        start=(kc == 0), stop=(kc == 7)),
                         r=[B("oT_sb"), B("wout%d" % g2)], w=[bb], c=0.22)
                S.op("dve", lambda e, g2=g2, bank=bank: e.tensor_tensor(out=yr[P, g2 * 512:(g2 + 1) * 512], in0=bank[P, :], in1=xt[P, g2 * 512:(g2 + 1) * 512], op=ALU.add),
                     r=[bb, xb], w=[yrb], c=0.7)
                if g2 == 0:
                    yield
            S.op("act", lambda e: e.activation(out=junk[P, :], in_=yr[P, :], func=AF.Square, accum_out=ss2[P, 0:1]), r=[yrb], w=[B("ss2")] + JALL, c=1.1)
            rsqrt_pool(rs2[P, :], ss2[P, :], 1.0 / D, P, slice(0, 1), B("ss2"), B("rs2"))
            yo = yout[ysl]; yob = B("yout%d" % ysl)
            if i >= NTILE or j == NTOT - 1:
                S.op("dve", lambda e: e.scalar_tensor_tensor(out=yo[P, :], in0=yr[P, :], scalar=rs2[P, 0:1], in1=fng_bc[P, :],
                                                            op0=ALU.mult, op1=ALU.mult), r=[yrb, B("rs2"), B("fng_bc")], w=[yob], c=1.1)
            else:
                S.op("act", lambda e: e.activation(out=yo[P, :], in_=yr[P, :], func=AF.Copy, scale=rs2[P, 0:1]), r=[yrb, B("rs2")], w=[yob], c=1.25)
                for hh in range(2):
                    S.op("pool", lambda e, hh=hh: e.tensor_tensor(out=yo[P, hh * 512:(hh + 1) * 512], in0=yo[P, hh * 512:(hh + 1) * 512],
                                                                 in1=fng_bc[P, hh * 512:(hh + 1) * 512], op=ALU.mult), r=[yob, B("fng_bc")], w=[yob], c=1.27)
            dst = yp[i * 128:(i + 1) * 128, :] if i < NTILE else ys[:, :]
            stores.append(S.dma("sp", lambda e: [e.dma_start(out=dst, in_=yo[P, :])], key=("y", ysl), r=[yob]))
            yield

        def stage_B(j, last):
            sl = j % 2
            qTb = B("qT%d" % sl); kTb = B("kT%d" % sl)
            for G in range(2):
                hs = slice(4 * G, 4 * G + 4)
                for hl in range(4):
                    h = 4 * G + hl
                    S.op("pe", lambda e, h=h, hl=hl: e.matmul(at_ps[0:64, hl, 0:64], lhsT=kT[sl][:, h, 0:64], rhs=qT[sl][:, h, 0:64], start=True, stop=True),
                         r=[qTb, kTb], w=[B("at_ps")], c=0.06)
                    S.op("pe", lambda e, h=h, hl=hl: e.matmul(at_ps[:, hl, 64:128], lhsT=kT[sl][:, h, :], rhs=qT[sl][:, h, 64:128], start=True, stop=True),
                         r=[qTb, kTb], w=[B("at_ps")], c=0.07)
                S.op("dve", lambda e, G=G: e.tensor_tensor(out=Am[G][0:64, :, 0:64], in0=at_ps[0:64, :, 0:64],
                                                          in1=mask_v[0:64, 0:64].unsqueeze(1).to_broadcast([64, 4, 64]), op=ALU.mult),
                     r=[B("at_ps"), B("cf")], w=[B("Am%d" % G)], c=0.35)
                S.op("dve", lambda e, G=G: e.tensor_tensor(out=Am[G][:, :, 64:128], in0=at_ps[:, :, 64:128],
                                                          in1=mask_v[:, 64:128].unsqueeze(1).to_broadcast([128, 4, 64]), op=ALU.mult),
                     r=[B("at_ps"), B("cf")], w=[B("Am%d" % G)], c=0.45)
                S.op("dve", lambda e, hs=hs: e.tensor_tensor(out=Sd_bf[:, hs, :], in0=S_sb[:, hs, :],
                                                            in1=evec[sl][:, 0, hs].unsqueeze(2).to_broadcast([128, 4, 128]), op=ALU.mult),
                     r=[B("S%d" % G), B("evec%d_%d" % (sl, G))], w=[B("Sd%d" % G)], c=0.62)
                yield
                for hl in range(4):
                    h = 4 * G + hl
                    S.op("pe", lambda e, h=h, hl=hl, G=G: e.matmul(o_ps[G][:, hl, :], lhsT=Am[G][:, hl, :], rhs=v_bf[sl][:, h * 128:(h + 1) * 128], start=True, stop=False),
                         r=[B("Am%d" % G), B("v_bf%d" % sl)], w=[B("o_ps%d" % G)], c=0.07)
                    S.op("pe", lambda e, h=h, hl=hl, G=G: e.matmul(o_ps[G][:, hl, :], lhsT=qT[sl][:, h, :], rhs=Sd_bf[:, h, :], start=False, stop=True),
                         r=[qTb, B("Sd%d" % G)], w=[B("o_ps%d" % G)], c=0.07)
                for hl in range(4):
                    h = 4 * G + hl
                    if G == 0:
                        ktok = qr_bf[sl][:, 1, hl * 128:(hl + 1) * 128]; kb = B("qr_bf%d" % sl)
                    else:
                        ktok = hk_bf[sl][:, hl * 128:(hl + 1) * 128]; kb = B("hk_bf%d" % sl)
                    S.op("pe", lambda e, h=h, hl=hl, ktok=ktok: e.matmul(at_ps[:, hl, :], lhsT=ktok, rhs=v_bf[sl][:, h * 128:(h + 1) * 128], start=True, stop=True),
                         r=[kb, B("v_bf%d" % sl)], w=[B("at_ps")], c=0.07)
                for hl in range(4):
                    h = 4 * G + hl
                    S.op("dve", lambda e, h=h, hl=hl: e.scalar_tensor_tensor(out=S_sb[:, h, :], in0=S_sb[:, h, :], scalar=evec[sl][:, 1, h:h + 1], in1=at_ps[:, hl, :],
                                                                            op0=ALU.mult, op1=ALU.add),
                         r=[B("S%d" % G), B("evec%d_%d" % (sl, G)), B("at_ps")], w=[B("S%d" % G)], c=0.37)
                if last:
                    dst = (nrp if G == 0 else nhp).rearrange("h d v -> d h v")
                    stores.append(S.dma("sp", lambda e, dst=dst, hs=hs: [e.dma_start(out=dst, in_=S_sb[:, hs, :])], key=("Sout", G), r=[B("S%d" % G)]))
                norm_group(G, o_ps[G], B("o_ps%d" % G), 128, sl, j % 2)
                yield

        def sample_pre():
            sl = 0
            P = slice(0, NS)
            for rnd in range(2):
                for jj in range(4):
                    h = rnd * 4 + jj
                    src = qr_bf[sl][P, 0, h * 128:(h + 1) * 128] if h < 4 else hq_bf[sl][P, (h - 4) * 128:(h - 3) * 128]
                    sbn = ("qr_bf%d" % sl) if h < 4 else ("hq_bf%d" % sl)
                    S.op("pe", lambda e, jj=jj, src=src: e.transpose(out=tT[:, jj, 0:NS], in_=src, identity=ident_bf[P, 0:NS]),
                         r=[B(sbn), B("ident_bf")], w=[B("tT")], c=0.055)
                S.op("dve", lambda e, rnd=rnd: e.tensor_copy(out=qTs[:, rnd * 4:(rnd + 1) * 4, :], in_=tT[:, :, 0:NS]), r=[B("tT")], w=[B("qTs")], c=0.3)
            S.op("dve", lambda e: e.tensor_copy(out=decT[:, 0:4, :], in_=gam_v.unsqueeze(2).to_broadcast([128, 4, NS])), r=[B("cf")], w=[B("decT")], c=0.2)
            for h in range(4):
                S.op("pe", lambda e, h=h: e.transpose(out=fT_ps[:, h, :], in_=f_sb[:, h * 128:(h + 1) * 128], identity=idf_v[P, 0:NS]),
                     r=[B("f_sb"), B("cf")], w=[B("fT_ps")], c=0.055)
            S.op("dve", lambda e: e.tensor_copy(out=decT[:, 4:8, :], in_=fT_ps), r=[B("fT_ps")], w=[B("decT")], c=0.2)
            S.op("act", lambda e: e.activation(out=vS, in_=v_bf[sl][P, :], func=AF.Copy), r=[B("v_bf%d" % sl)], w=[B("vS")], c=1.0)
            S.op("pool", lambda e: e.tensor_copy(out=kS[:, 0:512], in_=qr_bf[sl][P, 1, :]), r=[B("qr_bf%d" % sl)], w=[B("kS")], c=0.8)
            S.op("pool", lambda e: e.tensor_copy(out=kS[:, 512:1024], in_=hk_bf[sl][P, :]), r=[B("hk_bf%d" % sl)], w=[B("kS")], c=0.8)
            S.op("act", lambda e: e.activation(out=gSa, in_=gate[sl][P, 0:512], func=AF.Copy), r=[B("gate%d" % sl)], w=[B("gSa")], c=0.6)
            S.op("act", lambda e: e.activation(out=gSb, in_=gate[sl][P, 512:1024], func=AF.Copy), r=[B("gate%d" % sl)], w=[B("gSb")], c=0.6)

        def sample_stage(j):
            sl = j % 2
            P = slice(0, NS)
            R32 = slice(32, 32 + NS)
            bufs["os_g0"] = bufs["sq_sb"]; bufs["os_g1"] = bufs["th_sb"]
            yield
            kvb = [(pp[0][:].rearrange("p (a b) -> p a b", a=4), B("pp0")), (pp[1][:].rearrange("p (a b) -> p a b", a=4), B("pp1")),
                   (u_ps[:].rearrange("p (a b) -> p a b", a=4), B("u_ps")), (o_ps0[:], B("o_ps0"))]

            LOOKAHEAD = 6

            def chunk_io(c):
                h = c // 4; q = c % 4
                src = (sret if h < 4 else shg)[q * 4:(q + 1) * 4, h % 4, :, :].rearrange("t d v -> d t v")
                dst = (nrs if h < 4 else nhs)[q * 4:(q + 1) * 4, h % 4, :, :].rearrange("t d v -> d t v")
                return src, dst

            def load_chunk(c):
                slot = c % NCH
                src, _ = chunk_io(c)
                tb = [B("Ssc%d_%d" % (slot, t)) for t in range(4)]
                S.dma("sp", lambda e: [e.dma_start(out=Ssc[slot], in_=src)], key=("Ssc", slot), w=tb)

            def head_pre(h):
                km = kmask[h % 2]; kmb = B("kmask%d" % (h % 2))
                ktok = kS[:, h * 128:(h + 1) * 128]; kb = B("kS")
                S.op("pool", lambda e: e.tensor_tensor(out=km, in0=ktok.unsqueeze(1).to_broadcast([NS, NS, 128]),
                                                      in1=idf_v[R32, 32:32 + NS].unsqueeze(2).to_broadcast([NS, NS, 128]), op=ALU.mult),
                     r=[kb, B("cf")], w=[kmb], c=3.6)
                qm = qmask[h % 2]; qmb = B("qmask%d" % (h % 2))
                S.op("pool", lambda e: e.tensor_tensor(out=qm[:], in0=qTs[:, h, :].unsqueeze(2).to_broadcast([128, NS, NS]),
                                                      in1=eye_v, op=ALU.mult),
                     r=[B("qTs"), B("cf")], w=[qmb], c=0.6)

            def do_chunk(c):
                h = c // 4; q = c % 4
                slot = c % NCH
                Sc = Ssc[slot]
                tb = [B("Ssc%d_%d" % (slot, t)) for t in range(4)]
                km = kmask[h % 2]; kmb = B("kmask%d" % (h % 2))
                bv, bkb = kvb[c % 4]
                _, dst = chunk_io(c)
                for tt in range(4):
                    t = q * 4 + tt
                    S.op("pe", lambda e, t=t, tt=tt: e.matmul(bv[:, tt, :], lhsT=km[:, t, :], rhs=vS[:, h * 128:(h + 1) * 128], start=True, stop=True),
                         r=[kmb, B("vS")], w=[bkb], c=0.07)
                for tt in range(4):
                    t = q * 4 + tt
                    S.op("dve", lambda e, t=t, tt=tt: e.scalar_tensor_tensor(out=Sc[:, tt, :], in0=Sc[:, tt, :], scalar=decT[:, h, t:t + 1], in1=bv[:, tt, :],
                                                                            op0=ALU.mult, op1=ALU.add),
                         r=[tb[tt], B("decT"), bkb], w=[tb[tt]], c=0.37)
                stores.append(S.dma("sp", lambda e: [e.dma_start(out=dst, in_=Sc)], key=("Ssc", slot), r=tb))
                Sf = Sbf[h % 2]; Sfb = B("Sbf%d" % (h % 2))
                S.op("act", lambda e: e.activation(out=Sf[:, q * 4:(q + 1) * 4, :], in_=Sc, func=AF.Copy), r=tb, w=[Sfb], c=0.6)

            def head_post(h):
                qm = qmask[h % 2]; qmb = B("qmask%d" % (h % 2))
                Sf = Sbf[h % 2]; Sfb = B("Sbf%d" % (h % 2))
                for t in range(NS):
                    S.op("pe", lambda e, t=t: e.matmul(os_ps[P, :], lhsT=qm[:, t, :], rhs=Sf[:, t, :], start=(t == 0), stop=(t == NS - 1)),
                         r=[qmb, Sfb], w=[B("os_ps")], c=0.07)
                S.op("act", lambda e: e.activation(out=os_half[h // 4][:, h % 4, :], in_=os_ps[P, :], func=AF.Copy), r=[B("os_ps")], w=[B("os_g%d" % (h // 4))], c=0.36)

            NCHUNK = 32
            for c in range(min(LOOKAHEAD, NCHUNK)):
                load_chunk(c)
            head_pre(0)
            for c in range(NCHUNK):
                h = c // 4; q = c % 4
                do_chunk(c)
                if c + LOOKAHEAD < NCHUNK:
                    load_chunk(c + LOOKAHEAD)
                if q == 1 and h >= 1:
                    head_post(h - 1)
                if q == 2 and h + 1 < 8:
                    head_pre(h + 1)
                if q == 3:
                    yield
            head_post(7)
            yield "chunks_done"
            slp = _gs
            S.op("act", lambda e: e.activation(out=gate[slp][P, 0:512], in_=gSa, func=AF.Copy), r=[B("gSa")], w=[B("gate%d" % slp)], c=0.6)
            S.op("act", lambda e: e.activation(out=gate[slp][P, 512:1024], in_=gSb, func=AF.Copy), r=[B("gSb")], w=[B("gate%d" % slp)], c=0.6)
            xsl_ = _k0
            S.dma("sp", lambda e: [e.dma_start(out=x_sb[xsl_][P, :], in_=xs[:, :])], key=("x", xsl_), w=[B("x_sb%d" % xsl_)], us=0.3)
            for G in range(2):
                norm_group(G, os_half[G], B("os_g%d" % G), NS, slp, slp)
            yr_s = qk_sb[:, :, :].rearrange("p a b -> p (a b)")
            for _ in tail_stage(j, force_sample=True, par=slp, xsl=xsl_, yr=yr_s, yrb=B("qk_sb")):
                yield

        def drain(g):
            for _ in g:
                pass

        def step(g):
            if g is None:
                return False
            try:
                next(g)
                return True
            except StopIteration:
                return False

        for j in range(min(3, NTOT)):
            load_x(j)
        load_cs(0)
        drain(stage_A1(0))
        if NTOT > 1:
            drain(stage_A1(1))
        for g in G_ORDER:
            prep_group("in", g)
        prep_group("out", 0)
        prep_group("out", 1)
        def inherit(buf, srcs):
            buf.last_w = None
            buf.readers = []
            for sname in srcs:
                sbf = B(sname)
                if sbf.last_w is not None:
                    buf.readers.append(sbf.last_w)
                buf.readers.extend(sbf.readers)

        for n in ("yr_sb", "yout0", "yout1", "sqo_sb", "oc_bf0", "vS", "kS", "gSa", "gSb"):
            inherit(bufs[n], tuple("wsl%d" % n_ for n_ in range(0, 8)))
        for c_ in range(NCH):
            for t_ in range(4):
                inherit(bufs["Ssc%d_%d" % (c_, t_)], tuple("wsl%d" % n_ for n_ in range(8, 16)))

        PATTERN = "sbscscs" + "bsbascsbsasbs"
        for it in range(-1, NTOT + 1):
            gA1 = stage_A1(it + 2) if it + 2 < NTOT else None
            gA2 = stage_A2(it + 1) if it + 1 < NTOT else None
            gB = stage_B(it, last=(it == NTOT - 1)) if 1 <= it < NTOT else None
            gC = tail_stage(it - 1) if 1 <= it - 1 < NTOT else None
            if it + 2 < NTOT:
                load_cs(it + 2)
            gens = {"a": gA1, "s": gA2, "b": gB, "c": gC}
            for ch in PATTERN:
                step(gens[ch])
            for g in (gC, gB, gA2, gA1):
                if g is not None:
                    drain(g)
            if it == -1:
                sample_pre()
            if it + 3 < NTOT:
                load_x(it + 3)
            if it == NTOT - 2:
                gS_ = sample_stage(n_tiles)
                for r_ in gS_:
                    if r_ == "chunks_done":
                        break
            if it == NTOT - 1:
                drain(gS_)

        if USE_LIST_SCHED:
            est = S.schedule()
            print("[sched] estimated us:", round(est, 1))
        with nc.Block() as block:
            S.emit(block, final_waits=stores)
    return nc


_CACHE = {}


def _get_program():
    if "nc" not in _CACHE:
        _CACHE["nc"] = build_program()
    return _CACHE["nc"]


def make_in_maps(x_prompt, x_sample, state_ret, state_hgrn, norm_g, w_in, ret_norm_g, hg_norm_g, hg_lb, w_out, final_norm_g, cores=range(NCORES)):
    f = lambda a: np.ascontiguousarray(np.asarray(a, dtype=np.float32))
    x_prompt = f(x_prompt); x_sample = f(x_sample); state_ret = f(state_ret); state_hgrn = f(state_hgrn)
    cf, cs, cb = _make_consts()
    ng = np.ascontiguousarray(f(norm_g)[0].reshape(8, 128).T)
    gn = np.ascontiguousarray(np.concatenate([f(ret_norm_g)[0], f(hg_norm_g)[0]]).reshape(8, 128).T)
    shared = {
        "cs": cs, "cb": cb,
        "w_in": f(w_in)[0], "w_out": f(w_out)[0], "ng": ng, "gn": gn, "hglb": f(hg_lb),
        "fng": f(final_norm_g).reshape(1, D), "cf": cf,
    }
    maps = []
    for c in cores:
        m = dict(shared)
        m["xp"] = x_prompt[c]
        m["xs"] = np.ascontiguousarray(x_sample[c * NS:(c + 1) * NS, 0, :])
        m["sret"] = np.ascontiguousarray(state_ret[0, c * NS:(c + 1) * NS])
        m["shg"] = np.ascontiguousarray(state_hgrn[0, c * NS:(c + 1) * NS])
        maps.append(m)
    return maps


def kernel(x_prompt, x_sample, state_ret, state_hgrn, norm_g, w_in, ret_norm_g, hg_norm_g, hg_lb, w_out, final_norm_g):
    nc = _get_program()
    maps = make_in_maps(x_prompt, x_sample, state_ret, state_hgrn, norm_g, w_in, ret_norm_g, hg_norm_g, hg_lb, w_out, final_norm_g)
    res = run_bass_kernel_spmd(nc, maps, core_ids=list(range(NCORES)))
    R = res.results
    y_prompt = np.stack([R[c]["yp"] for c in range(NCORES)], axis=0).astype(np.float32)
    y_sample = np.concatenate([R[c]["ys"] for c in range(NCORES)], axis=0).reshape(NCORES * NS, 1, D).astype(np.float32)
    nrp = np.stack([R[c]["nrp"] for c in range(NCORES)], axis=0)[None].astype(np.float32)
    nhp = np.stack([R[c]["nhp"] for c in range(NCORES)], axis=0)[None].astype(np.float32)
    nrs = np.concatenate([R[c]["nrs"] for c in range(NCORES)], axis=0)[None].astype(np.float32)
    nhs = np.concatenate([R[c]["nhs"] for c in range(NCORES)], axis=0)[None].astype(np.float32)
    return (y_prompt, y_sample, nrp, nhp, nrs, nhs)
```

```python
import numpy as np
import ml_dtypes
from contextlib import ExitStack

import concourse.bass as bass
import concourse.mybir as mybir
from concourse.bass_utils import run_bass_kernel_spmd

F32 = mybir.dt.float32
BF16 = mybir.dt.bfloat16
AF = mybir.ActivationFunctionType
ALU = mybir.AluOpType
AX = mybir.AxisListType

D = 1024
L = 2048
NTILE = 16
NS = 16
EPS = 1e-6
NCORES = 8
USE_LIST_SCHED = True

FAC_OFF = 0
FAC_N = 16
ERET_OFF = FAC_OFF + FAC_N
ERET_N = 8
GAM_OFF = ERET_OFF + ERET_N
GAM_N = 4
MASK_OFF = GAM_OFF + GAM_N
EYE_OFF = MASK_OFF + 128
IDF_OFF = EYE_OFF + 256
NCF = IDF_OFF + 128


def _make_consts():
    cf = np.zeros((128, NCF), np.float32)
    half = 64
    freqs = 1.0 / (10000.0 ** (np.arange(half, dtype=np.float64) / half))
    p = np.arange(128)
    cs = np.zeros((128, 17, 2, 64), np.float32)
    for i in range(17):
        pos = (i * 128 + p).astype(np.float64) if i < 16 else np.full(128, 16384.0, np.float64)
        ang = pos[:, None] * freqs[None, :]
        cs[:, i, 0, :] = np.cos(ang)
        cs[:, i, 1, :] = np.sin(ang)
    gam = 1.0 - np.exp2(-5.0 - np.arange(4, dtype=np.float64))
    t = p.astype(np.float64)
    fac = np.zeros((128, 2, 8), np.float64)
    for h in range(4):
        fac[:, 0, h] = gam[h] ** (t - 127.0)
        fac[:, 0, 4 + h] = (128.0 ** -0.5) * gam[h] ** (127.0 - t)
        fac[:, 1, h] = 1.0
        fac[:, 1, 4 + h] = 128.0 ** -0.5
    cf[:, FAC_OFF:FAC_OFF + FAC_N] = fac.reshape(128, -1)
    eret = np.zeros((128, 2, 4), np.float64)
    eret[:, 0, :] = gam[None, :] ** 128.0
    eret[:, 1, :] = gam[None, :] ** 128.0
    cf[:, ERET_OFF:ERET_OFF + ERET_N] = eret.reshape(128, -1)
    cf[:, GAM_OFF:GAM_OFF + 4] = gam[None, :]
    s = p[:, None]
    tt = p[None, :]
    cf[:, MASK_OFF:MASK_OFF + 128] = (s <= tt)
    cb = np.zeros((128, 128 + 260), np.float32)
    cb[:, 0:128] = np.eye(128, dtype=np.float32)
    m1 = np.zeros((128, 128), np.float32)
    m1[(s > 63) & (s <= tt)] = 1.0
    m1[(s > tt) & (s <= 63)] = -1.0
    cb[:, 128:256] = m1
    cb[:, 256:384] = (s > tt)
    cb[:, 384] = (p <= 63)
    cb[:, 385] = 1.0
    cf[:, EYE_OFF:EYE_OFF + 256] = np.eye(16, dtype=np.float32).reshape(1, 256)
    cf[:, IDF_OFF:IDF_OFF + 128] = np.eye(128, dtype=np.float32)
    cs_t = np.ascontiguousarray(cs.reshape(128, 17, 128).transpose(1, 0, 2))
    return cf, cs_t, cb.astype(ml_dtypes.bfloat16)


class Buf:
    __slots__ = ("name", "last_w", "readers")

    def __init__(self, name):
        self.name = name
        self.last_w = None
        self.readers = []


class Op:
    __slots__ = ("eng", "fn", "deps", "sem", "val", "need_inc", "is_dma", "ninc", "name", "alldeps", "cost", "done_lat", "tab", "idx")


DEFAULT_COST = {"pe": 0.24, "act": 0.72, "dve": 0.66, "pool": 1.3, "sp": 0.6}


class Sched:
    ENGS = ("pe", "act", "dve", "pool", "sp")
    ENGATTR = {"pe": "tensor", "act": "scalar", "dve": "vector", "pool": "gpsimd", "sp": "sync"}

    def __init__(self, nc, stack, same_engine_sync=True):
        self.nc = nc
        self.stack = stack
        self.ops = {e: [] for e in self.ENGS}
        self.eng_sem = {e: stack.enter_context(nc.semaphore("sem_" + e)) for e in self.ENGS}
        self.dma_sems = {}
        self.dma_cnt = {}
        self.same_engine_sync = same_engine_sync
        self.nops = 0

    def op(self, eng, fn, r=(), w=(), name="", c=None, tab=None):
        o = Op()
        o.eng = eng; o.fn = fn; o.is_dma = False; o.need_inc = False; o.name = name
        o.sem = None; o.val = None; o.ninc = 1
        o.cost = DEFAULT_COST[eng] if c is None else c
        o.done_lat = 0.0; o.tab = tab; o.idx = self.nops; self.nops += 1
        self._deps(o, r, w)
        self.ops[eng].append(o)
        return o

    def dma(self, eng, fn, key, r=(), w=(), n=1, name="", us=1.5):
        o = Op()
        o.eng = eng; o.fn = fn; o.is_dma = True; o.need_inc = True; o.name = name
        o.cost = 0.45 * n; o.done_lat = us; o.tab = None; o.idx = self.nops; self.nops += 1
        if key not in self.dma_sems:
            self.dma_sems[key] = self.stack.enter_context(self.nc.semaphore("dsem_%d" % len(self.dma_sems)))
            self.dma_cnt[key] = 0
        self.dma_cnt[key] += 16 * n
        o.sem = self.dma_sems[key]; o.val = self.dma_cnt[key]; o.ninc = n
        self._deps(o, r, w)
        self.ops[eng].append(o)
        return o

    def _deps(self, o, r, w):
        deps = []
        for b in r:
            if b.last_w is not None:
                deps.append(b.last_w)
        for b in w:
            if b.last_w is not None:
                deps.append(b.last_w)
            deps.extend(b.readers)
        seen = set(); out = []
        o.alldeps = []
        for d in deps:
            if id(d) in seen or d is o:
                continue
            seen.add(id(d))
            o.alldeps.append(d)
            if (not d.is_dma) and (not o.is_dma) and d.eng == o.eng and (o.eng == "pe" or not self.same_engine_sync):
                continue
            if not d.is_dma:
                d.need_inc = True
            out.append(d)
        o.deps = out
        for b in r:
            b.readers.append(o)
        for b in w:
            b.last_w = o
            b.readers = []

    def schedule(self, sem_lat=0.15, table_cost=1.3):
        import os as _os
        sem_lat = float(_os.environ.get("SCHED_SEMLAT", "1.0"))
        SAME_LAT = float(_os.environ.get("SCHED_SAMELAT", "0.02"))
        allops = []
        for e in self.ENGS:
            allops.extend(self.ops[e])
        allops.sort(key=lambda o: o.idx)
        ndeps = {}
        users = {}
        for o in allops:
            ndeps[id(o)] = len(o.alldeps)
            for d in o.alldeps:
                users.setdefault(id(d), []).append(o)
        fin = {}
        self.sim_start = {}
        self.sim_fin = fin
        blev = {}
        for o in reversed(allops):
            m = 0.0
            for u in users.get(id(o), ()):
                m = max(m, blev[id(u)] + (sem_lat if u.eng != o.eng else 0.0))
            blev[id(o)] = m + o.cost + (o.done_lat + 2.0 if o.is_dma else 0.0)
        EPS_T = float(_os.environ.get("SCHED_EPS", "0.0"))
        ready = {e: [] for e in self.ENGS}
        for o in allops:
            if ndeps[id(o)] == 0:
                ready[o.eng].append((0.0, o))
        free = {e: 0.0 for e in self.ENGS}
        cur_tab = [None]
        dma_free = [0.0]
        order = {e: [] for e in self.ENGS}
        remaining = len(allops)
        WINDOW = int(_os.environ.get("SCHED_WINDOW", "150"))
        sched = set()
        oldest = 0
        while remaining:
            while oldest < len(allops) and id(allops[oldest]) in sched:
                oldest += 1
            best = None
            for e in self.ENGS:
                for (rt, o) in ready[e]:
                    if o.idx > allops[oldest].idx + WINDOW:
                        continue
                    st = max(free[e], rt)
                    if e == "act" and o.tab is not None and o.tab != cur_tab[0]:
                        st += table_cost
                    key = ((round(st / EPS_T) if EPS_T > 0 else st), -blev[id(o)] if EPS_T > 0 else 0.0, o.idx)
                    if best is None or key < best[0]:
                        best = (key, e, rt, o)
            assert best is not None
            e, rt, o = best[1], best[2], best[3]
            st = max(free[e], rt)
            if e == "act" and o.tab is not None and o.tab != cur_tab[0]:
                st += table_cost
            ready[e].remove((rt, o))
            if e == "act" and o.tab is not None:
                cur_tab[0] = o.tab
            self.sim_start[id(o)] = st
            free[e] = st + o.cost
            if o.is_dma:
                tx0 = max(st + o.cost, dma_free[0])
                dma_free[0] = tx0 + o.done_lat
                fin[id(o)] = dma_free[0] + 2.0
            else:
                fin[id(o)] = st + o.cost
            order[e].append(o)
            sched.add(id(o))
            remaining -= 1
            for u in users.get(id(o), ()):
                ndeps[id(u)] -= 1
                if ndeps[id(u)] == 0:
                    rt_u = max(fin[id(d)] + (sem_lat if d.eng != u.eng else SAME_LAT) for d in u.alldeps)
                    ready[u.eng].append((rt_u, u))
        self.ops = order
        return max(free.values())

    def emit(self, block, final_waits=()):
        for e in self.ENGS:
            c = 0
            for o in self.ops[e]:
                if o.is_dma:
                    continue
                if o.need_inc:
                    c += 1
                    o.sem = self.eng_sem[e]; o.val = c
        finals = [(o.sem, o.val) for o in final_waits]
        for e in self.ENGS:
            ops = self.ops[e]

            def body(eng, ops=ops, e=e):
                known = {}
                for o in ops:
                    need = {}
                    for d in o.deps:
                        k = id(d.sem)
                        if known.get(k, 0) >= d.val:
                            continue
                        if k not in need or need[k][1] < d.val:
                            need[k] = (d.sem, d.val)
                    for k, (s, v) in need.items():
                        eng.wait_ge(s, v)
                        known[k] = v
                    res = o.fn(eng)
                    if o.is_dma:
                        assert len(res) == o.ninc, (o.name, len(res), o.ninc)
                        for ins in res:
                            ins.then_inc(o.sem, 16)
                    elif o.need_inc:
                        res.then_inc(o.sem, 1)
                if e == "sp":
                    best = {}
                    for (s, v) in finals:
                        if id(s) not in best or best[id(s)][1] < v:
                            best[id(s)] = (s, v)
                    for (s, v) in best.values():
                        eng.wait_ge(s, v)

            getattr(block, self.ENGATTR[e])(body)


def build_program(n_tiles=NTILE, do_sample=True):
    nc = bass.Bass("TRN2", target_bir_lowering=False)

    def din(name, shape, dt=F32):
        return nc.dram_tensor(name, shape, dt, kind="ExternalInput").ap()

    def dout(name, shape, dt=F32):
        return nc.dram_tensor(name, shape, dt, kind="ExternalOutput").ap()

    xp = din("xp", [L, D])
    xs = din("xs", [NS, D])
    sret = din("sret", [NS, 4, 128, 128])
    shg = din("shg", [NS, 4, 128, 128])
    w_in = din("w_in", [D, 4096])
    w_out = din("w_out", [D, D])
    ng = din("ng", [128, 8])
    gn = din("gn", [128, 8])
    hglb = din("hglb", [2, 512])
    fng = din("fng", [1, D])
    cfd = din("cf", [128, NCF])
    csd = din("cs", [17, 128, 128])
    cbd = din("cb", [128, 388], BF16)

    yp = dout("yp", [L, D])
    ys = dout("ys", [NS, D])
    nrp = dout("nrp", [4, 128, 128])
    nhp = dout("nhp", [4, 128, 128])
    nrs = dout("nrs", [NS, 4, 128, 128])
    nhs = dout("nhs", [NS, 4, 128, 128])

    with ExitStack() as st:
        S = Sched(nc, st)
        bufs = {}

        def sb(name, shape, dt=F32):
            t = st.enter_context(nc.sbuf_tensor(name, shape, dt))
            bufs[name] = Buf(name)
            return t

        def ps(name, shape, dt=F32):
            t = st.enter_context(nc.psum_tensor(name, shape, dt))
            bufs[name] = Buf(name)
            return t

        def B(name):
            return bufs[name]

        w_in_bf = sb("w_in_bf", [128, 8, 4096], BF16)
        for g in range(8):
            bufs["win%d" % g] = Buf("win%d" % g)
        w_out_bf = sb("w_out_bf", [128, 8, 1024], BF16)
        for g in range(2):
            bufs["wout%d" % g] = Buf("wout%d" % g)
        wst = [sb("wst%d" % i, [128, 8, 512], F32) for i in range(2)]
        cf = sb("cf_sb", [128, NCF], F32)
        bufs["cf"] = bufs["cf_sb"]
        cs_sb = [sb("cs_sb%d" % i, [128, 2, 64], F32) for i in range(2)]
        cb_sb = sb("cb_sb", [128, 388], BF16)
        bufs["ident_bf"] = bufs["cb_sb"]; bufs["M12_bf"] = bufs["cb_sb"]
        ident_bf = cb_sb[:, 0:128]
        M12_bf = cb_sb[:, 128:388]
        mhalf = sb("mhalf", [128, 8], F32)
        ng_sb = sb("ng_sb", [128, 8], F32)
        gn_sb = sb("gn_sb", [128, 8], F32)
        negc1 = sb("negc1", [128, 512], F32)
        fng_bc = sb("fng_bc", [128, D], F32)
        NX = 4
        x_sb = [sb("x_sb%d" % i, [128, D], F32) for i in range(NX)]
        junk = st.enter_context(nc.sbuf_tensor("junk", [128, D], BF16))
        for q_ in range(8):
            bufs["junk%d" % q_] = Buf("junk%d" % q_)
        JALL = [bufs["junk%d" % q_] for q_ in range(8)]
        ss = sb("ss", [128, 1], F32)
        rstd = sb("rstd", [128, 1], F32)
        h_bf = sb("h_bf", [128, D], BF16)
        hT = [sb("hT%d" % i, [128, 8, 128], BF16) for i in range(2)]
        qk_sb = sb("qk_sb", [128, 2, 512], F32)
        rt1 = sb("rt1", [128, 8, 64], F32)
        rt2 = sb("rt2", [128, 8, 64], F32)
        qr_bf = [sb("qr_bf%d" % i, [128, 2, 512], BF16) for i in range(2)]
        v_bf = [sb("v_bf%d" % i, [128, 1024], BF16) for i in range(2)]
        gate = [sb("gate%d" % i, [128, 1024], F32) for i in range(2)]
        sq_sb = sb("sq_sb", [128, 512], F32)
        th_sb = sb("th_sb", [128, 512], F32)
        kin_sb = sb("kin_sb", [128, 512], F32)
        logf_sb = sb("logf_sb", [128, 512], F32)
        lhi = sb("lhi", [128, 512], BF16)
        llo = sb("llo", [128, 512], BF16)
        eq_sb = th_sb
        bufs["eq_sb"] = bufs["th_sb"]
        hq_bf = [sb("hq_bf%d" % i, [128, 512], BF16) for i in range(2)]
        hk2_bf = sb("hk2_bf", [128, 512], BF16)
        hk_bf = [sb("hk_bf%d" % i, [128, 512], BF16) for i in range(2)]
        qT = [sb("qT%d" % i, [128, 8, 128], BF16) for i in range(2)]
        kT = [sb("kT%d" % i, [128, 8, 128], BF16) for i in range(2)]
        evec = [sb("evec%d" % i, [128, 2, 8], F32) for i in range(2)]
        for i_ in range(2):
            for g_ in range(2):
                bufs["evec%d_%d" % (i_, g_)] = Buf("evec%d_%d" % (i_, g_))
                bufs["qT%d_%d" % (i_, g_)] = Buf("qT%d_%d" % (i_, g_))
                bufs["kT%d_%d" % (i_, g_)] = Buf("kT%d_%d" % (i_, g_))
        Am = [sb("Am%d" % i, [128, 4, 128], BF16) for i in range(2)]
        S_sb = sb("S_sb", [128, 8, 128], F32)
        bufs["S0"] = Buf("S0"); bufs["S1"] = Buf("S1")
        Sd_bf = sb("Sd_bf", [128, 8, 128], BF16)
        bufs["Sd0"] = Buf("Sd0"); bufs["Sd1"] = Buf("Sd1")
        ssq = sb("ssq", [128, 8], F32)
        rs = sb("rs", [128, 8], F32)
        for g_ in range(2):
            bufs["ssq%d" % g_] = Buf("ssq%d" % g_); bufs["rs%d" % g_] = Buf("rs%d" % g_)
        oT_sb = sb("oT_sb", [128, 8, 128], BF16)
        ss2 = sb("ss2", [128, 1], F32)
        rs2 = sb("rs2", [128, 1], F32)
        wst0_flat = wst[0][:].rearrange("p a b -> p (a b)")
        yr_sb = wst0_flat[:, 0:1024]
        yout = [wst0_flat[:, 1024:2048], wst0_flat[:, 2048:3072]]
        sqo_sb = wst0_flat[:, 3072:3584]
        oc1_t = sb("oc1_t", [128, 1024], BF16)
        oc_bf = [wst0_flat[:, 3584:4096].bitcast(BF16), oc1_t[:, :]]
        bufs["oc_bf1"] = bufs["oc1_t"]
        for n in ("yr_sb", "yout0", "yout1", "sqo_sb", "oc_bf0"):
            bufs[n] = Buf(n)
        vS = sqo_sb[0:NS, :].bitcast(BF16)
        kS = sqo_sb[32:32 + NS, :].bitcast(BF16)
        gSa = sqo_sb[64:64 + NS, :]
        gSb = sqo_sb[96:96 + NS, :]
        for n in ("vS", "kS", "gSa", "gSb"):
            bufs[n] = Buf(n)
        wst1_flat = wst[1][:].rearrange("p a b -> p (a b)")
        NCH = 8
        Ssc = [wst1_flat[:, c_ * 512:(c_ + 1) * 512].rearrange("p (t v) -> p t v", t=4) for c_ in range(NCH)]
        for c_ in range(NCH):
            for t_ in range(4):
                bufs["Ssc%d_%d" % (c_, t_)] = Buf("Ssc")
        decT = sb("decT", [128, 8, NS], F32)
        qTs = sb("qTs", [128, 8, NS], BF16)
        qmask = [sb("qmask%d" % i, [128, NS, NS], BF16) for i in range(2)]
        f_sb = logf_sb[0:NS, :]
        bufs["f_sb"] = bufs["logf_sb"]
        _gs = (n_tiles - 1) % 2
        Sbf = [gate[_gs][:, :].bitcast(BF16).rearrange("p (t d) -> p t d", t=NS),
               qk_sb[:, :, :].rearrange("p a b -> p (a b)").bitcast(BF16).rearrange("p (t d) -> p t d", t=NS)]
        bufs["Sbf0"] = bufs["gate%d" % _gs]
        bufs["Sbf1"] = bufs["qk_sb"]
        os_half = [sq_sb[0:NS, :].rearrange("p (h d) -> p h d", h=4), th_sb[0:NS, :].rearrange("p (h d) -> p h d", h=4)]
        _xs = (n_tiles - 1) % NX
        os_sb = x_sb[_xs][0:NS, :].rearrange("p (h d) -> p h d", h=8)
        bufs["os_sb"] = bufs["x_sb%d" % _xs]
        bufs["os_g0"] = Buf("os_g0"); bufs["os_g1"] = Buf("os_g1")
        _k0 = (n_tiles + 1) % NX; _k1 = (n_tiles + 2) % NX
        kmask = [x_sb[_k0][0:NS, :].bitcast(BF16).rearrange("p (t d) -> p t d", t=NS),
                 x_sb[_k1][0:NS, :].bitcast(BF16).rearrange("p (t d) -> p t d", t=NS)]
        bufs["kmask0"] = bufs["x_sb%d" % _k0]
        bufs["kmask1"] = bufs["x_sb%d" % _k1]

        pp = [ps("pp0", [128, 512], F32), ps("pp1", [128, 512], F32)]
        tp = ps("tp", [128, 8, 128], BF16)
        tTev = ps("tTev", [128, 512], F32)
        tT = tTev[:, 0:256].bitcast(BF16).rearrange("p (a b) -> p a b", a=4)
        ev_ps = tTev[:, 256:272].rearrange("p (h c) -> p h c", h=4)
        bufs["tT"] = bufs["tTev"]; bufs["ev_ps"] = bufs["tTev"]
        u_ps = ps("u_ps", [128, 512], F32)
        at_ps = ps("at_ps", [128, 4, 128], F32)
        o_ps0 = ps("o_ps0", [128, 4, 128], F32)
        misc = ps("misc", [128, 512], F32)
        o_ps = [o_ps0, misc[:, :].rearrange("p (a b) -> p a b", a=4)]
        bufs["o_ps1"] = bufs["misc"]
        os_ps = misc[:, 16:144]
        fT_ps = misc[:, 144:208].rearrange("p (h t) -> p h t", h=4)
        bufs["os_ps"] = bufs["misc"]; bufs["fT_ps"] = bufs["misc"]

        fac_v = cf[:, FAC_OFF:FAC_OFF + FAC_N].rearrange("p (w j) -> p w j", w=2)
        eret_v = cf[:, ERET_OFF:ERET_OFF + ERET_N].rearrange("p (c h) -> p c h", c=2)
        gam_v = cf[:, GAM_OFF:GAM_OFF + 4]
        mask_v = cf[:, MASK_OFF:MASK_OFF + 128]
        eye_v = cf[:, EYE_OFF:EYE_OFF + 256].rearrange("p (a b) -> p a b", a=16)
        idf_v = cf[:, IDF_OFF:IDF_OFF + 128]
        M1_bf = M12_bf[:, 0:128]
        M2_bf = M12_bf[:, 128:256]
        sel_bf = M12_bf[:, 256:260]

        stores = []
        NTOT = n_tiles + (1 if do_sample else 0)

        def tile_id(j):
            return NTILE if j == 0 else j - 1

        S.dma("sp", lambda e: [e.dma_start(out=cf[:], in_=cfd)], key="cf", w=[B("cf")], us=1.2)
        S.dma("sp", lambda e: [e.dma_start(out=ng_sb[:], in_=ng)], key="ng", w=[B("ng_sb")], us=0.3)
        S.dma("sp", lambda e: [e.dma_start(out=gn_sb[:], in_=gn)], key="gn", w=[B("gn_sb")], us=0.3)
        S.dma("sp", lambda e: [e.dma_start(out=sq_sb[:], in_=hglb[0:1, :].partition_broadcast(128))], key="lb0", w=[B("sq_sb")], us=0.3)
        S.dma("sp", lambda e: [e.dma_start(out=th_sb[:], in_=hglb[1:2, :].partition_broadcast(128))], key="lb1", w=[B("th_sb")], us=0.3)
        S.dma("sp", lambda e: [e.dma_start(out=fng_bc[:], in_=fng[0:1, :].partition_broadcast(128))], key="fng", w=[B("fng_bc")], us=0.3)

        S.dma("sp", lambda e: [e.dma_start(out=cb_sb[:], in_=cbd)], key="cb", w=[B("cb_sb")], us=0.3)
        S.op("pool", lambda e: e.memset(mhalf[:], -0.5), w=[B("mhalf")], c=0.5)
        for i in range(2):
            S.op("dve", lambda e, i=i: e.tensor_copy(out=evec[i][:, :, 0:4], in_=eret_v), r=[B("cf")], w=[B("evec%d_0" % i)], c=0.62)
            S.op("pool", lambda e, i=i: e.memset(Am[i][:], 0.0), w=[B("Am%d" % i)], c=0.5)
        S.op("pool", lambda e: e.memset(S_sb[:], 0.0), w=[B("S0"), B("S1")], c=0.5)
        S.op("dve", lambda e: e.tensor_tensor(out=negc1[:], in0=th_sb[:], in1=sq_sb[:], op=ALU.subtract),
             r=[B("sq_sb"), B("th_sb")], w=[B("negc1")], c=0.62)
        S.op("act", lambda e: e.activation(out=negc1[:], in_=negc1[:], func=AF.Tanh, scale=0.5), r=[B("negc1")], w=[B("negc1")], c=0.62, tab="silu")
        S.op("dve", lambda e: e.tensor_scalar(out=negc1[:], in0=negc1[:], scalar1=1.0, scalar2=-0.25, op0=ALU.add, op1=ALU.mult),
             r=[B("negc1")], w=[B("negc1")], c=0.62)

        RSQRT_ON_ACT = True

        def rsqrt_pool(dst, src, scale, P, cols, rb, wb):
            if RSQRT_ON_ACT:
                S.op("act", lambda e: e.activation(out=dst, in_=src, func=AF.Ln, scale=scale, bias=EPS), r=[rb], w=[wb], c=0.22, tab="lnexp")
                S.op("act", lambda e: e.activation(out=dst, in_=dst, func=AF.Exp, scale=-0.5), r=[wb], w=[wb], c=0.22, tab="lnexp")
            else:
                S.op("pool", lambda e: e.tensor_scalar(out=dst, in0=src, scalar1=scale, scalar2=EPS, op0=ALU.mult, op1=ALU.add), r=[rb], w=[wb], c=0.3)
                S.op("pool", lambda e: e.tensor_tensor(out=dst, in0=dst, in1=mhalf[P, cols], op=ALU.pow), r=[wb, B("mhalf")], w=[wb], c=0.3)

        wprep_cnt = [0]

        NWS = 16
        for n_ in range(NWS):
            bufs["wsl%d" % n_] = Buf("wsl")

        def prep_group(kind, g):
            if kind == "in":
                srcw = w_in[:, g * 512:(g + 1) * 512].rearrange("(kc p) n -> p kc n", p=128)
                dstbuf = B("win%d" % g)
                scl = ng_sb; sclb = B("ng_sb")
            else:
                srcw = w_out[:, g * 512:(g + 1) * 512].rearrange("(kc p) n -> p kc n", p=128)
                dstbuf = B("wout%d" % g)
                scl = gn_sb; sclb = B("gn_sb")
            engs = ["act", "dve", "pool", "act", "dve", "act", "dve", "pool"]
            for kc in range(8):
                n_ = wprep_cnt[0] % NWS
                wprep_cnt[0] += 1
                stg = wst[n_ // 8]
                k8 = n_ % 8
                sbuf_ = B("wsl%d" % n_)
                S.dma("sp", lambda e, stg=stg, k8=k8, kc=kc: [e.dma_start(out=stg[:, k8, :], in_=srcw[:, kc, :])],
                      key=("wsl", n_), w=[sbuf_], us=1.15)
                if kind == "in":
                    dst = w_in_bf[:, kc, g * 512:(g + 1) * 512]
                else:
                    dst = w_out_bf[:, kc, g * 512:(g + 1) * 512]
                en = engs[kc]
                srcv = stg[:, k8, :]
                if en == "act":
                    S.op("act", lambda e, dst=dst, kc=kc, srcv=srcv: e.activation(out=dst, in_=srcv, func=AF.Copy, scale=scl[:, kc:kc + 1]),
                         r=[sbuf_, sclb], w=[dstbuf], c=0.6)
                else:
                    S.op(en, lambda e, dst=dst, kc=kc, srcv=srcv: e.tensor_scalar(out=dst, in0=srcv, scalar1=scl[:, kc:kc + 1], scalar2=1.0,
                                                                                op0=ALU.mult, op1=ALU.mult),
                         r=[sbuf_, sclb], w=[dstbuf], c=(0.62 if en == "dve" else 1.27))

        G_ORDER = [5, 4, 3, 7, 0, 1, 2, 6]
        pp_ctr = [0]

        def next_pp():
            n = pp_ctr[0]; pp_ctr[0] += 1
            return pp[n % 2], B("pp%d" % (n % 2))

        def load_x(j):
            i = tile_id(j)
            NT = 128 if i < NTILE else NS
            sl = j % NX
            src = xp[i * 128:(i + 1) * 128, :] if i < NTILE else xs[:, :]
            S.dma("sp", lambda e: [e.dma_start(out=x_sb[sl][0:NT, :], in_=src)], key=("x", sl), w=[B("x_sb%d" % sl)])

        def load_cs(j):
            i = tile_id(j)
            sl = j % 2
            S.dma("sp", lambda e: [e.dma_start(out=cs_sb[sl][:].rearrange("p c f -> p (c f)"), in_=csd[i])], key=("cs", sl), w=[B("cs_sb%d" % sl)], us=0.3)

        def stage_A1(j):
            i = tile_id(j)
            NT = 128 if i < NTILE else NS
            xt = x_sb[j % NX]; xb = B("x_sb%d" % (j % NX))
            hTt = hT[j % 2]; hTb = B("hT%d" % (j % 2))
            P = slice(0, NT)
            S.op("act", lambda e: e.activation(out=junk[P, :], in_=xt[P, :], func=AF.Square, accum_out=ss[P, 0:1]), r=[xb], w=[B("ss")] + JALL, c=1.1)
            rsqrt_pool(rstd[P, :], ss[P, :], 1.0 / D, P, slice(0, 1), B("ss"), B("rstd"))
            S.op("pool", lambda e: e.tensor_scalar(out=h_bf[P, :], in0=xt[P, :], scalar1=rstd[P, 0:1], scalar2=1.0, op0=ALU.mult, op1=ALU.mult),
                 r=[xb, B("rstd")], w=[B("h_bf")], c=1.15)
            yield
            for kc in range(8):
                S.op("pe", lambda e, kc=kc: e.transpose(out=tp[:, kc, 0:NT], in_=h_bf[P, kc * 128:(kc + 1) * 128], identity=ident_bf[P, 0:NT]),
                     r=[B("h_bf"), B("ident_bf")], w=[B("tp")], c=0.055)
            S.op("dve", lambda e: e.tensor_copy(out=hTt[:, :, 0:NT], in_=tp[:, :, 0:NT]), r=[B("tp")], w=[hTb], c=0.7)
            yield

        def stage_A2(j):
            i = tile_id(j)
            NT = 128 if i < NTILE else NS
            sample = i >= NTILE
            sl = j % 2
            hTt = hT[j % 2]; hTb = B("hT%d" % (j % 2))
            cst = cs_sb[j % 2]; csb = B("cs_sb%d" % (j % 2))
            P = slice(0, NT)
            facw = 1 if sample else 0

            def proj(g):
                bank, bb = next_pp()
                for kc in range(8):
                    S.op("pe", lambda e, kc=kc: e.matmul(bank[P, :], lhsT=hTt[:, kc, 0:NT], rhs=w_in_bf[:, kc, g * 512:(g + 1) * 512],
                                                        start=(kc == 0), stop=(kc == 7)),
                         r=[hTb, B("win%d" % g)], w=[bb], c=0.22)
                return bank, bb

            bank, bb = proj(5)
            S.op("act", lambda e, bank=bank: e.activation(out=th_sb[P, :], in_=bank[P, :], func=AF.Tanh, scale=0.5), r=[bb], w=[B("th_sb")], c=0.62, tab="silu")
            S.op("dve", lambda e: e.scalar_tensor_tensor(out=kin_sb[P, :], in0=th_sb[P, :], scalar=1.0, in1=negc1[P, :], op0=ALU.subtract, op1=ALU.mult),
                 r=[B("th_sb"), B("negc1")], w=[B("kin_sb")], c=0.66)
            yield
            bank, bb = proj(4)
            S.op("act", lambda e, bank=bank: e.activation(out=sq_sb[P, :], in_=bank[P, :], func=AF.Silu), r=[bb], w=[B("sq_sb")], c=0.62, tab="silu")
            yield
            bank, bb = proj(3)
            S.op("act", lambda e, bank=bank: e.activation(out=gate[sl][P, 0:512], in_=bank[P, :], func=AF.Silu), r=[bb], w=[B("gate%d" % sl)], c=0.62, tab="silu")
            yield
            bank, bb = proj(7)
            S.op("act", lambda e, bank=bank: e.activation(out=gate[sl][P, 512:1024], in_=bank[P, :], func=AF.Silu), r=[bb], w=[B("gate%d" % sl)], c=0.62, tab="silu")
            if not sample:
                S.op("act", lambda e: e.activation(out=logf_sb[:], in_=kin_sb[:], func=AF.Ln, scale=-1.0, bias=1.0), r=[B("kin_sb")], w=[B("logf_sb")], c=0.62, tab="lnexp")
                S.op("dve", lambda e: e.tensor_copy(out=lhi[:], in_=logf_sb[:]), r=[B("logf_sb")], w=[B("lhi")], c=0.62)
                S.op("pool", lambda e: e.tensor_tensor(out=llo[:], in0=logf_sb[:], in1=lhi[:], op=ALU.subtract), r=[B("logf_sb"), B("lhi")], w=[B("llo")], c=1.27)
            else:
                S.op("dve", lambda e: e.tensor_copy(out=hq_bf[sl][P, :], in_=sq_sb[P, :]), r=[B("sq_sb")], w=[B("hq_bf%d" % sl)], c=0.62)
                S.op("dve", lambda e: e.tensor_copy(out=hk_bf[sl][P, :], in_=kin_sb[P, :]), r=[B("kin_sb")], w=[B("hk_bf%d" % sl)], c=0.62)
                S.op("dve", lambda e: e.tensor_scalar(out=f_sb, in0=kin_sb[P, :], scalar1=-1.0, scalar2=1.0, op0=ALU.mult, op1=ALU.add),
                     r=[B("kin_sb")], w=[B("f_sb")], c=0.62)
            yield
            bank, bb = proj(0)
            S.op("dve", lambda e, bank=bank: e.tensor_tensor(
                out=qk_sb[P, 0, :].rearrange("p (h d) -> p h d", h=4), in0=bank[P, :].rearrange("p (h d) -> p h d", h=4),
                in1=fac_v[P, facw, 0:4].unsqueeze(2).to_broadcast([NT, 4, 128]), op=ALU.mult), r=[bb, B("cf")], w=[B("qk_sb")], c=0.62)
            if not sample:
                S.op("pe", lambda e: e.matmul(u_ps[:], lhsT=M1_bf, rhs=lhi[:], start=True, stop=False), r=[B("M12_bf"), B("lhi")], w=[B("u_ps")], c=0.3)
                S.op("pe", lambda e: e.matmul(u_ps[:], lhsT=M1_bf, rhs=llo[:], start=False, stop=True), r=[B("M12_bf"), B("llo")], w=[B("u_ps")], c=0.3)
                for h in range(4):
                    S.op("pe", lambda e, h=h: e.matmul(ev_ps[:, h, :], lhsT=lhi[:, h * 128:(h + 1) * 128], rhs=sel_bf, start=True, stop=False),
                         r=[B("M12_bf"), B("lhi")], w=[B("ev_ps")], c=0.06)
                    S.op("pe", lambda e, h=h: e.matmul(ev_ps[:, h, :], lhsT=llo[:, h * 128:(h + 1) * 128], rhs=sel_bf, start=False, stop=True),
                         r=[B("M12_bf"), B("llo")], w=[B("ev_ps")], c=0.06)
                S.op("act", lambda e: e.activation(out=eq_sb[:], in_=u_ps[:], func=AF.Exp), r=[B("u_ps")], w=[B("eq_sb")], c=0.62, tab="lnexp")
                S.op("act", lambda e: e.activation(out=u_ps[:], in_=u_ps[:], func=AF.Exp, scale=-1.0), r=[B("u_ps")], w=[B("u_ps")], c=0.62, tab="lnexp")
                S.op("act", lambda e: e.activation(out=evec[sl][:, :, 4:8], in_=ev_ps[:, :, 0:2].rearrange("p h c -> p c h"), func=AF.Exp),
                     r=[B("ev_ps")], w=[B("evec%d_1" % sl)], c=0.15, tab="lnexp")
                S.op("dve", lambda e: e.tensor_tensor(out=hk2_bf[:], in0=u_ps[:], in1=kin_sb[:], op=ALU.mult), r=[B("u_ps"), B("kin_sb")], w=[B("hk2_bf")], c=0.62)
                S.op("pool", lambda e: e.tensor_tensor(out=hq_bf[sl][:], in0=eq_sb[:], in1=sq_sb[:], op=ALU.mult), r=[B("eq_sb"), B("sq_sb")], w=[B("hq_bf%d" % sl)], c=1.27)
            yield
            bank, bb = proj(1)
            S.op("dve", lambda e, bank=bank: e.tensor_tensor(
                out=qk_sb[P, 1, :].rearrange("p (h d) -> p h d", h=4), in0=bank[P, :].rearrange("p (h d) -> p h d", h=4),
                in1=fac_v[P, facw, 4:8].unsqueeze(2).to_broadcast([NT, 4, 128]), op=ALU.mult), r=[bb, B("cf")], w=[B("qk_sb")], c=0.62)
            if not sample:
                S.op("pe", lambda e: e.matmul(u_ps[:], lhsT=M2_bf, rhs=lhi[:], start=True, stop=False), r=[B("M12_bf"), B("lhi")], w=[B("u_ps")], c=0.3)
                S.op("pe", lambda e: e.matmul(u_ps[:], lhsT=M2_bf, rhs=llo[:], start=False, stop=True), r=[B("M12_bf"), B("llo")], w=[B("u_ps")], c=0.3)
                S.op("act", lambda e: e.activation(out=u_ps[:], in_=u_ps[:], func=AF.Exp), r=[B("u_ps")], w=[B("u_ps")], c=0.62, tab="lnexp")
                S.op("dve", lambda e: e.tensor_tensor(out=hk_bf[sl][:], in0=u_ps[:], in1=kin_sb[:], op=ALU.mult), r=[B("u_ps"), B("kin_sb")], w=[B("hk_bf%d" % sl)], c=0.62)
            qv = qk_sb[P, :, :].rearrange("p a (h t f) -> p (a h) t f", h=4, t=2)
            ov = qr_bf[sl][P, :, :].rearrange("p a (h t f) -> p (a h) t f", h=4, t=2)
            cosb = cst[P, 0, :].unsqueeze(1).to_broadcast([NT, 8, 64])
            sinb = cst[P, 1, :].unsqueeze(1).to_broadcast([NT, 8, 64])
            x1 = qv[:, :, 0, :]; x2 = qv[:, :, 1, :]
            rb = [B("qk_sb"), csb]
            S.op("pool", lambda e: e.tensor_tensor(out=rt1[P], in0=x1, in1=cosb, op=ALU.mult), r=rb, w=[B("rt1")], c=1.27)
            S.op("pool", lambda e: e.tensor_tensor(out=rt2[P], in0=x2, in1=sinb, op=ALU.mult), r=rb, w=[B("rt2")], c=1.27)
            S.op("pool", lambda e: e.tensor_tensor(out=ov[:, :, 0, :], in0=rt1[P], in1=rt2[P], op=ALU.subtract),
                 r=[B("rt1"), B("rt2")], w=[B("qr_bf%d" % sl)], c=1.27)
            S.op("pool", lambda e: e.tensor_tensor(out=rt1[P], in0=x1, in1=sinb, op=ALU.mult), r=rb, w=[B("rt1")], c=1.27)
            S.op("pool", lambda e: e.tensor_tensor(out=rt2[P], in0=x2, in1=cosb, op=ALU.mult), r=rb, w=[B("rt2")], c=1.27)
            S.op("pool", lambda e: e.tensor_tensor(out=ov[:, :, 1, :], in0=rt1[P], in1=rt2[P], op=ALU.add),
                 r=[B("rt1"), B("rt2")], w=[B("qr_bf%d" % sl)], c=1.27)
            yield
            def tround(srcs, sbn, dst, dstb):
                for jj in range(4):
                    S.op("pe", lambda e, jj=jj: e.transpose(out=tT[:, jj, :], in_=srcs[jj], identity=ident_bf),
                         r=[B(sbn), B("ident_bf")], w=[B("tT")], c=0.055)
                S.op("dve", lambda e: e.tensor_copy(out=dst, in_=tT), r=[B("tT")], w=[dstb], c=0.4)

            bank, bb = proj(2)
            S.op("act", lambda e, bank=bank: e.activation(out=v_bf[sl][P, 0:512], in_=bank[P, :], func=AF.Copy), r=[bb], w=[B("v_bf%d" % sl)], c=0.62)
            if not sample:
                tround([hq_bf[sl][:, jj * 128:(jj + 1) * 128] for jj in range(4)], "hq_bf%d" % sl, qT[sl][:, 4:8, :], B("qT%d_1" % sl))
            yield
            if not sample:
                tround([hk2_bf[:, jj * 128:(jj + 1) * 128] for jj in range(4)], "hk2_bf", kT[sl][:, 4:8, :], B("kT%d_1" % sl))
            yield
            bank, bb = proj(6)
            S.op("act", lambda e, bank=bank: e.activation(out=v_bf[sl][P, 512:1024], in_=bank[P, :], func=AF.Copy), r=[bb], w=[B("v_bf%d" % sl)], c=0.62)
            if not sample:
                tround([qr_bf[sl][:, 0, jj * 128:(jj + 1) * 128] for jj in range(4)], "qr_bf%d" % sl, qT[sl][:, 0:4, :], B("qT%d_0" % sl))
            yield
            if not sample:
                tround([qr_bf[sl][:, 1, jj * 128:(jj + 1) * 128] for jj in range(4)], "qr_bf%d" % sl, kT[sl][:, 0:4, :], B("kT%d_0" % sl))
            yield

        def norm_group(G, src, srcb, NT, gsl, ocs):
            P = slice(0, NT)
            hs = slice(4 * G, 4 * G + 4)
            for hl in range(4):
                h = 4 * G + hl
                S.op("act", lambda e, h=h, hl=hl: e.activation(out=junk[P, hl * 128:(hl + 1) * 128], in_=src[:, hl, :], func=AF.Square, accum_out=ssq[P, h:h + 1]),
                     r=[srcb], w=[B("ssq%d" % G), B("junk%d" % hl)], c=0.3)
            rsqrt_pool(rs[P, hs], ssq[P, hs], 1.0 / 128, P, hs, B("ssq%d" % G), B("rs%d" % G))
            for hl in range(4):
                h = 4 * G + hl
                S.op("dve", lambda e, h=h, hl=hl: e.scalar_tensor_tensor(
                    out=oc_bf[ocs][P, h * 128:(h + 1) * 128], in0=src[:, hl, :], scalar=rs[P, h:h + 1], in1=gate[gsl][P, h * 128:(h + 1) * 128],
                    op0=ALU.mult, op1=ALU.mult), r=[srcb, B("rs%d" % G), B("gate%d" % gsl)], w=[B("oc_bf%d" % ocs)], c=0.37)

        def tail_stage(j, force_sample=False, par=None, xsl=None, yr=None, yrb=None):
            i = NTILE if force_sample else tile_id(j)
            NT = 128 if i < NTILE else NS
            if yr is None:
                yr = yr_sb; yrb = B("yr_sb")
            P = slice(0, NT)
            ysl = (j % 2) if par is None else par
            ocs = ysl
            for kc in range(8):
                S.op("pe", lambda e, kc=kc: e.transpose(out=tp[:, kc, 0:NT], in_=oc_bf[ocs][P, kc * 128:(kc + 1) * 128], identity=ident_bf[P, 0:NT]),
                     r=[B("oc_bf%d" % ocs), B("ident_bf")], w=[B("tp")], c=0.055)
            S.op("act", lambda e: e.activation(out=oT_sb[:, :, 0:NT], in_=tp[:, :, 0:NT], func=AF.Copy), r=[B("tp")], w=[B("oT_sb")], c=1.1)
            yield
            xs_ = (j % NX) if xsl is None else xsl
            xt = x_sb[xs_]; xb = B("x_sb%d" % xs_)
            for g2 in range(2):
                bank, bb = next_pp()
                for kc in range(8):
                    S.op("pe", lambda e, kc=kc, g2=g2, bank=bank: e.matmul(bank[P, :], lhsT=oT_sb[:, kc, 0:NT], rhs=w_out_bf[:, kc, g2 * 512:(g2 + 1) * 512],
                                                                        start=(kc == 0), stop=(kc == 7)),
                         r=[B("oT_sb"), B("wout%d" % g2)], w=[bb], c=0.22)
                S.op("dve", lambda e, g2=g2, bank=bank: e.tensor_tensor(out=yr[P, g2 * 512:(g2 + 1) * 512], in0=bank[P, :], in1=xt[P, g2 * 512:(g2 + 1) * 512], op=ALU.add),
                     r=[bb, xb], w=[yrb], c=0.7)
                if g2 == 0:
                    yield
            S.op("act", lambda e: e.activation(out=junk[P, :], in_=yr[P, :], func=AF.Square, accum_out=ss2[P, 0:1]), r=[yrb], w=[B("ss2")] + JALL, c=1.1)
            rsqrt_pool(rs2[P, :], ss2[P, :], 1.0 / D, P, slice(0, 1), B("ss2"), B("rs2"))
            yo = yout[ysl]; yob = B("yout%d" % ysl)
            if i >= NTILE or j == NTOT - 1:
                S.op("dve", lambda e: e.scalar_tensor_tensor(out=yo[P, :], in0=yr[P, :], scalar=rs2[P, 0:1], in1=fng_bc[P, :],
                                                            op0=ALU.mult, op1=ALU.mult), r=[yrb, B("rs2"), B("fng_bc")], w=[yob], c=1.1)
            else:
                S.op("act", lambda e: e.activation(out=yo[P, :], in_=yr[P, :], func=AF.Copy, scale=rs2[P, 0:1]), r=[yrb, B("rs2")], w=[yob], c=1.25)
                for hh in range(2):
                    S.op("pool", lambda e, hh=hh: e.tensor_tensor(out=yo[P, hh * 512:(hh + 1) * 512], in0=yo[P, hh * 512:(hh + 1) * 512],
                                                                 in1=fng_bc[P, hh * 512:(hh + 1) * 512], op=ALU.mult), r=[yob, B("fng_bc")], w=[yob], c=1.27)
            dst = yp[i * 128:(i + 1) * 128, :] if i < NTILE else ys[:, :]
            stores.append(S.dma("sp", lambda e: [e.dma_start(out=dst, in_=yo[P, :])], key=("y", ysl), r=[yob]))
            yield

        def stage_B(j, last):
            sl = j % 2
            for G in range(2):
                qTb = B("qT%d_%d" % (sl, G)); kTb = B("kT%d_%d" % (sl, G))
                hs = slice(4 * G, 4 * G + 4)
                for hl in range(4):
                    h = 4 * G + hl
                    S.op("pe", lambda e, h=h, hl=hl: e.matmul(at_ps[0:64, hl, 0:64], lhsT=kT[sl][:, h, 0:64], rhs=qT[sl][:, h, 0:64], start=True, stop=True),
                         r=[qTb, kTb], w=[B("at_ps")], c=0.06)
                    S.op("pe", lambda e, h=h, hl=hl: e.matmul(at_ps[:, hl, 64:128], lhsT=kT[sl][:, h, :], rhs=qT[sl][:, h, 64:128], start=True, stop=True),
                         r=[qTb, kTb], w=[B("at_ps")], c=0.07)
                S.op("dve", lambda e, G=G: e.tensor_tensor(out=Am[G][0:64, :, 0:64], in0=at_ps[0:64, :, 0:64],
                                                          in1=mask_v[0:64, 0:64].unsqueeze(1).to_broadcast([64, 4, 64]), op=ALU.mult),
                     r=[B("at_ps"), B("cf")], w=[B("Am%d" % G)], c=0.35)
                S.op("dve", lambda e, G=G: e.tensor_tensor(out=Am[G][:, :, 64:128], in0=at_ps[:, :, 64:128],
                                                          in1=mask_v[:, 64:128].unsqueeze(1).to_broadcast([128, 4, 64]), op=ALU.mult),
                     r=[B("at_ps"), B("cf")], w=[B("Am%d" % G)], c=0.45)
                S.op("dve", lambda e, hs=hs: e.tensor_tensor(out=Sd_bf[:, hs, :], in0=S_sb[:, hs, :],
                                                            in1=evec[sl][:, 0, hs].unsqueeze(2).to_broadcast([128, 4, 128]), op=ALU.mult),
                     r=[B("S%d" % G), B("evec%d_%d" % (sl, G))], w=[B("Sd%d" % G)], c=0.62)
                yield
                for hl in range(4):
                    h = 4 * G + hl
                    S.op("pe", lambda e, h=h, hl=hl, G=G: e.matmul(o_ps[G][:, hl, :], lhsT=Am[G][:, hl, :], rhs=v_bf[sl][:, h * 128:(h + 1) * 128], start=True, stop=False),
                         r=[B("Am%d" % G), B("v_bf%d" % sl)], w=[B("o_ps%d" % G)], c=0.07)
                    S.op("pe", lambda e, h=h, hl=hl, G=G: e.matmul(o_ps[G][:, hl, :], lhsT=qT[sl][:, h, :], rhs=Sd_bf[:, h, :], start=False, stop=True),
                         r=[qTb, B("Sd%d" % G)], w=[B("o_ps%d" % G)], c=0.07)
                for hl in range(4):
                    h = 4 * G + hl
                    if G == 0:
                        ktok = qr_bf[sl][:, 1, hl * 128:(hl + 1) * 128]; kb = B("qr_bf%d" % sl)
                    else:
                        ktok = hk_bf[sl][:, hl * 128:(hl + 1) * 128]; kb = B("hk_bf%d" % sl)
                    S.op("pe", lambda e, h=h, hl=hl, ktok=ktok: e.matmul(at_ps[:, hl, :], lhsT=ktok, rhs=v_bf[sl][:, h * 128:(h + 1) * 128], start=True, stop=True),
                         r=[kb, B("v_bf%d" % sl)], w=[B("at_ps")], c=0.07)
                for hl in range(4):
                    h = 4 * G + hl
                    S.op("dve", lambda e, h=h, hl=hl: e.scalar_tensor_tensor(out=S_sb[:, h, :], in0=S_sb[:, h, :], scalar=evec[sl][:, 1, h:h + 1], in1=at_ps[:, hl, :],
                                                                            op0=ALU.mult, op1=ALU.add),
                         r=[B("S%d" % G), B("evec%d_%d" % (sl, G)), B("at_ps")], w=[B("S%d" % G)], c=0.37)
                if last:
                    dst = (nrp if G == 0 else nhp).rearrange("h d v -> d h v")
                    stores.append(S.dma("sp", lambda e, dst=dst, hs=hs: [e.dma_start(out=dst, in_=S_sb[:, hs, :])], key=("Sout", G), r=[B("S%d" % G)]))
                norm_group(G, o_ps[G], B("o_ps%d" % G), 128, sl, j % 2)
                yield

        def sample_pre():
            sl = 0
            P = slice(0, NS)
            for rnd in range(2):
                for jj in range(4):
                    h = rnd * 4 + jj
                    src = qr_bf[sl][P, 0, h * 128:(h + 1) * 128] if h < 4 else hq_bf[sl][P, (h - 4) * 128:(h - 3) * 128]
                    sbn = ("qr_bf%d" % sl) if h < 4 else ("hq_bf%d" % sl)
                    S.op("pe", lambda e, jj=jj, src=src: e.transpose(out=tT[:, jj, 0:NS], in_=src, identity=ident_bf[P, 0:NS]),
                         r=[B(sbn), B("ident_bf")], w=[B("tT")], c=0.055)
                S.op("dve", lambda e, rnd=rnd: e.tensor_copy(out=qTs[:, rnd * 4:(rnd + 1) * 4, :], in_=tT[:, :, 0:NS]), r=[B("tT")], w=[B("qTs")], c=0.3)
            S.op("dve", lambda e: e.tensor_copy(out=decT[:, 0:4, :], in_=gam_v.unsqueeze(2).to_broadcast([128, 4, NS])), r=[B("cf")], w=[B("decT")], c=0.2)
            for h in range(4):
                S.op("pe", lambda e, h=h: e.transpose(out=fT_ps[:, h, :], in_=f_sb[:, h * 128:(h + 1) * 128], identity=idf_v[P, 0:NS]),
                     r=[B("f_sb"), B("cf")], w=[B("fT_ps")], c=0.055)
            S.op("dve", lambda e: e.tensor_copy(out=decT[:, 4:8, :], in_=fT_ps), r=[B("fT_ps")], w=[B("decT")], c=0.2)
            S.op("act", lambda e: e.activation(out=vS, in_=v_bf[sl][P, :], func=AF.Copy), r=[B("v_bf%d" % sl)], w=[B("vS")], c=1.0)
            S.op("pool", lambda e: e.tensor_copy(out=kS[:, 0:512], in_=qr_bf[sl][P, 1, :]), r=[B("qr_bf%d" % sl)], w=[B("kS")], c=0.8)
            S.op("pool", lambda e: e.tensor_copy(out=kS[:, 512:1024], in_=hk_bf[sl][P, :]), r=[B("hk_bf%d" % sl)], w=[B("kS")], c=0.8)
            S.op("act", lambda e: e.activation(out=gSa, in_=gate[sl][P, 0:512], func=AF.Copy), r=[B("gate%d" % sl)], w=[B("gSa")], c=0.6)
            S.op("act", lambda e: e.activation(out=gSb, in_=gate[sl][P, 512:1024], func=AF.Copy), r=[B("gate%d" % sl)], w=[B("gSb")], c=0.6)

        def sample_stage(j):
            sl = j % 2
            P = slice(0, NS)
            R32 = slice(32, 32 + NS)
            bufs["os_g0"] = bufs["sq_sb"]; bufs["os_g1"] = bufs["th_sb"]
            yield
            kvb = [(pp[0][:].rearrange("p (a b) -> p a b", a=4), B("pp0")), (pp[1][:].rearrange("p (a b) -> p a b", a=4), B("pp1")),
                   (u_ps[:].rearrange("p (a b) -> p a b", a=4), B("u_ps")), (o_ps0[:], B("o_ps0"))]

            LOOKAHEAD = 6

            def chunk_io(c):
                h = c // 4; q = c % 4
                src = (sret if h < 4 else shg)[q * 4:(q + 1) * 4, h % 4, :, :].rearrange("t d v -> d t v")
                dst = (nrs if h < 4 else nhs)[q * 4:(q + 1) * 4, h % 4, :, :].rearrange("t d v -> d t v")
                return src, dst

            def load_chunk(c):
                slot = c % NCH
                src, _ = chunk_io(c)
                tb = [B("Ssc%d_%d" % (slot, t)) for t in range(4)]
                S.dma("sp", lambda e: [e.dma_start(out=Ssc[slot], in_=src)], key=("Ssc", slot), w=tb)

            def head_pre(h):
                km = kmask[h % 2]; kmb = B("kmask%d" % (h % 2))
                ktok = kS[:, h * 128:(h + 1) * 128]; kb = B("kS")
                S.op("pool", lambda e: e.tensor_tensor(out=km, in0=ktok.unsqueeze(1).to_broadcast([NS, NS, 128]),
                                                      in1=idf_v[R32, 32:32 + NS].unsqueeze(2).to_broadcast([NS, NS, 128]), op=ALU.mult),
                     r=[kb, B("cf")], w=[kmb], c=3.6)
                qm = qmask[h % 2]; qmb = B("qmask%d" % (h % 2))
                S.op("pool", lambda e: e.tensor_tensor(out=qm[:], in0=qTs[:, h, :].unsqueeze(2).to_broadcast([128, NS, NS]),
                                                      in1=eye_v, op=ALU.mult),
                     r=[B("qTs"), B("cf")], w=[qmb], c=0.6)

            def do_chunk(c):
                h = c // 4; q = c % 4
                slot = c % NCH
                Sc = Ssc[slot]
                tb = [B("Ssc%d_%d" % (slot, t)) for t in range(4)]
                km = kmask[h % 2]; kmb = B("kmask%d" % (h % 2))
                bv, bkb = kvb[c % 4]
                _, dst = chunk_io(c)
                for tt in range(4):
                    t = q * 4 + tt
                    S.op("pe", lambda e, t=t, tt=tt: e.matmul(bv[:, tt, :], lhsT=km[:, t, :], rhs=vS[:, h * 128:(h + 1) * 128], start=True, stop=True),
                         r=[kmb, B("vS")], w=[bkb], c=0.07)
                for tt in range(4):
                    t = q * 4 + tt
                    S.op("dve", lambda e, t=t, tt=tt: e.scalar_tensor_tensor(out=Sc[:, tt, :], in0=Sc[:, tt, :], scalar=decT[:, h, t:t + 1], in1=bv[:, tt, :],
                                                                            op0=ALU.mult, op1=ALU.add),
                         r=[tb[tt], B("decT"), bkb], w=[tb[tt]], c=0.37)
                stores.append(S.dma("sp", lambda e: [e.dma_start(out=dst, in_=Sc)], key=("Ssc", slot), r=tb))
                Sf = Sbf[h % 2]; Sfb = B("Sbf%d" % (h % 2))
                S.op("act", lambda e: e.activation(out=Sf[:, q * 4:(q + 1) * 4, :], in_=Sc, func=AF.Copy), r=tb, w=[Sfb], c=0.6)

            def head_post(h):
                qm = qmask[h % 2]; qmb = B("qmask%d" % (h % 2))
                Sf = Sbf[h % 2]; Sfb = B("Sbf%d" % (h % 2))
                for t in range(NS):
                    S.op("pe", lambda e, t=t: e.matmul(os_ps[P, :], lhsT=qm[:, t, :], rhs=Sf[:, t, :], start=(t == 0), stop=(t == NS - 1)),
                         r=[qmb, Sfb], w=[B("os_ps")], c=0.07)
                S.op("act", lambda e: e.activation(out=os_half[h // 4][:, h % 4, :], in_=os_ps[P, :], func=AF.Copy), r=[B("os_ps")], w=[B("os_g%d" % (h // 4))], c=0.36)

            NCHUNK = 32
            for c in range(min(LOOKAHEAD, NCHUNK)):
                load_chunk(c)
            head_pre(0)
            for c in range(NCHUNK):
                h = c // 4; q = c % 4
                do_chunk(c)
                if c + LOOKAHEAD < NCHUNK:
                    load_chunk(c + LOOKAHEAD)
                if q == 1 and h >= 1:
                    head_post(h - 1)
                if q == 2 and h + 1 < 8:
                    head_pre(h + 1)
                if q == 3:
                    yield
            head_post(7)
            yield "chunks_done"
            slp = _gs
            S.op("act", lambda e: e.activation(out=gate[slp][P, 0:512], in_=gSa, func=AF.Copy), r=[B("gSa")], w=[B("gate%d" % slp)], c=0.6)
            S.op("act", lambda e: e.activation(out=gate[slp][P, 512:1024], in_=gSb, func=AF.Copy), r=[B("gSb")], w=[B("gate%d" % slp)], c=0.6)
            xsl_ = _k0
            S.dma("sp", lambda e: [e.dma_start(out=x_sb[xsl_][P, :], in_=xs[:, :])], key=("x", xsl_), w=[B("x_sb%d" % xsl_)], us=0.3)
            for G in range(2):
                norm_group(G, os_half[G], B("os_g%d" % G), NS, slp, slp)
            yr_s = qk_sb[:, :, :].rearrange("p a b -> p (a b)")
            for _ in tail_stage(j, force_sample=True, par=slp, xsl=xsl_, yr=yr_s, yrb=B("qk_sb")):
                yield

        def drain(g):
            for _ in g:
                pass

        def step(g):
            if g is None:
                return False
            try:
                next(g)
                return True
            except StopIteration:
                return False

        for j in range(min(3, NTOT)):
            load_x(j)
        load_cs(0)
        drain(stage_A1(0))
        if NTOT > 1:
            drain(stage_A1(1))
        for g in G_ORDER:
            prep_group("in", g)
        prep_group("out", 0)
        prep_group("out", 1)
        def inherit(buf, srcs):
            buf.last_w = None
            buf.readers = []
            for sname in srcs:
                sbf = B(sname)
                if sbf.last_w is not None:
                    buf.readers.append(sbf.last_w)
                buf.readers.extend(sbf.readers)

        for n in ("yr_sb", "yout0", "yout1", "sqo_sb", "oc_bf0", "vS", "kS", "gSa", "gSb"):
            inherit(bufs[n], tuple("wsl%d" % n_ for n_ in range(0, 8)))
        for c_ in range(NCH):
            for t_ in range(4):
                inherit(bufs["Ssc%d_%d" % (c_, t_)], tuple("wsl%d" % n_ for n_ in range(8, 16)))

        PATTERN = "sbscscs" + "bsbascsbsasbs"
        for it in range(-1, NTOT + 1):
            gA1 = stage_A1(it + 2) if it + 2 < NTOT else None
            gA2 = stage_A2(it + 1) if it + 1 < NTOT else None
            gB = stage_B(it, last=(it == NTOT - 1)) if 1 <= it < NTOT else None
            gC = tail_stage(it - 1) if 1 <= it - 1 < NTOT else None
            if it + 2 < NTOT:
                load_cs(it + 2)
            gens = {"a": gA1, "s": gA2, "b": gB, "c": gC}
            for ch in PATTERN:
                step(gens[ch])
            for g in (gC, gB, gA2, gA1):
                if g is not None:
                    drain(g)
            if it == -1:
                sample_pre()
            if it + 3 < NTOT:
                load_x(it + 3)
            if it == NTOT - 2:
                gS_ = sample_stage(n_tiles)
                for r_ in gS_:
                    if r_ == "chunks_done":
                        break
            if it == NTOT - 1:
                drain(gS_)

        if USE_LIST_SCHED:
            est = S.schedule()
            print("[sched] estimated us:", round(est, 1))
        with nc.Block() as block:
            S.emit(block, final_waits=stores)
    return nc


_CACHE = {}


def _get_program():
    if "nc" not in _CACHE:
        _CACHE["nc"] = build_program()
    return _CACHE["nc"]


def make_in_maps(x_prompt, x_sample, state_ret, state_hgrn, norm_g, w_in, ret_norm_g, hg_norm_g, hg_lb, w_out, final_norm_g, cores=range(NCORES)):
    f = lambda a: np.ascontiguousarray(np.asarray(a, dtype=np.float32))
    x_prompt = f(x_prompt); x_sample = f(x_sample); state_ret = f(state_ret); state_hgrn = f(state_hgrn)
    cf, cs, cb = _make_consts()
    ng = np.ascontiguousarray(f(norm_g)[0].reshape(8, 128).T)
    gn = np.ascontiguousarray(np.concatenate([f(ret_norm_g)[0], f(hg_norm_g)[0]]).reshape(8, 128).T)
    shared = {
        "cs": cs, "cb": cb,
        "w_in": f(w_in)[0], "w_out": f(w_out)[0], "ng": ng, "gn": gn, "hglb": f(hg_lb),
        "fng": f(final_norm_g).reshape(1, D), "cf": cf,
    }
    maps = []
    for c in cores:
        m = dict(shared)
        m["xp"] = x_prompt[c]
        m["xs"] = np.ascontiguousarray(x_sample[c * NS:(c + 1) * NS, 0, :])
        m["sret"] = np.ascontiguousarray(state_ret[0, c * NS:(c + 1) * NS])
        m["shg"] = np.ascontiguousarray(state_hgrn[0, c * NS:(c + 1) * NS])
        maps.append(m)
    return maps


def kernel(x_prompt, x_sample, state_ret, state_hgrn, norm_g, w_in, ret_norm_g, hg_norm_g, hg_lb, w_out, final_norm_g):
    nc = _get_program()
    maps = make_in_maps(x_prompt, x_sample, state_ret, state_hgrn, norm_g, w_in, ret_norm_g, hg_norm_g, hg_lb, w_out, final_norm_g)
    res = run_bass_kernel_spmd(nc, maps, core_ids=list(range(NCORES)))
    R = res.results
    y_prompt = np.stack([R[c]["yp"] for c in range(NCORES)], axis=0).astype(np.float32)
    y_sample = np.concatenate([R[c]["ys"] for c in range(NCORES)], axis=0).reshape(NCORES * NS, 1, D).astype(np.float32)
    nrp = np.stack([R[c]["nrp"] for c in range(NCORES)], axis=0)[None].astype(np.float32)
    nhp = np.stack([R[c]["nhp"] for c in range(NCORES)], axis=0)[None].astype(np.float32)
    nrs = np.concatenate([R[c]["nrs"] for c in range(NCORES)], axis=0)[None].astype(np.float32)
    nhs = np.concatenate([R[c]["nhs"] for c in range(NCORES)], axis=0)[None].astype(np.float32)
    return (y_prompt, y_sample, nrp, nhp, nrs, nhs)
```
